# Optimizing a Trainium2 kernel written in Bass

```python
import jax, jax.numpy as jnp
from jax import lax
import numpy as np

D_MODEL = 2048
BATCH = 2
SEQ = 8192
DEPTH = 4

D_MIX = D_MODEL
D_SGU = D_MIX // 2
SGU_GROUPS = 8
SGU_GROUP_DIM = D_SGU // SGU_GROUPS
SGU_CHUNK = 128
D_DN = D_MIX - D_SGU
DN_HEADS = 8
DN_HEAD_DIM = D_DN // DN_HEADS
DN_CHUNK = 64
CONV_WIDTH = 5
NORM_EPS = 1e-6
IN_SIZES = (D_SGU, D_SGU, D_SGU, 3 * D_DN, D_DN, DN_HEADS, DN_HEADS, DN_HEADS, DN_HEADS)
D_IN = 3 * D_SGU + 4 * D_DN + 4 * DN_HEADS

kernel_name = "hybrid_gmlp_gated_deltanet_encoder"


def _rmsnorm(x, w):
    x32 = x.astype(jnp.float32)
    y = x32 * lax.rsqrt(jnp.mean(x32 * x32, axis=-1, keepdims=True) + NORM_EPS)
    return (y * w.astype(jnp.float32)).astype(x.dtype)


def _layernorm(x, g, b):
    x32 = x.astype(jnp.float32)
    mu = jnp.mean(x32, axis=-1, keepdims=True)
    xc = x32 - mu
    var = jnp.mean(xc * xc, axis=-1, keepdims=True)
    y = xc * lax.rsqrt(var + NORM_EPS)
    return (y * g.astype(jnp.float32) + b.astype(jnp.float32)).astype(x.dtype)


def _l2norm(x):
    return x * lax.rsqrt(jnp.sum(x * x, axis=-1, keepdims=True) + NORM_EPS)


def _split_in(proj):
    offsets = [int(o) for o in np.cumsum(IN_SIZES)[:-1]]
    return jnp.split(proj, offsets, axis=-1)


def _short_conv(x, w):
    c = x.shape[-1]
    pad = CONV_WIDTH // 2
    return lax.conv_general_dilated(
        x, w[:, None, :].astype(x.dtype), window_strides=(1,),
        padding=((pad, pad),), dimension_numbers=("NWC", "WIO", "NWC"),
        feature_group_count=c)


def _gated_delta_chunked(q, k, v, g, beta):
    b_, h, s, dk = q.shape
    dv = v.shape[-1]
    nc = s // DN_CHUNK
    q = q.reshape(b_, h, nc, DN_CHUNK, dk)
    k = k.reshape(b_, h, nc, DN_CHUNK, dk)
    v = v.reshape(b_, h, nc, DN_CHUNK, dv)
    g_cum = jnp.cumsum(g.reshape(b_, h, nc, DN_CHUNK), axis=-1)
    beta = beta.reshape(b_, h, nc, DN_CHUNK)

    lower = jnp.tril(jnp.ones((DN_CHUNK, DN_CHUNK), dtype=bool))
    strict = jnp.tril(jnp.ones((DN_CHUNK, DN_CHUNK), dtype=bool), k=-1)
    diff = g_cum[..., :, None] - g_cum[..., None, :]
    decay = jnp.where(lower, jnp.exp(jnp.where(lower, diff, 0.0)), 0.0)

    k_beta = k * beta[..., None]
    m = jnp.where(strict, jnp.einsum("bhnid,bhnjd->bhnij", k_beta, k) * decay, 0.0)
    eye = jnp.eye(DN_CHUNK, dtype=q.dtype)
    t_inv = lax.linalg.triangular_solve(
        eye + m, jnp.broadcast_to(eye, m.shape), left_side=True, lower=True,
        unit_diagonal=True)
    u = jnp.einsum("bhnij,bhnjd->bhnid", t_inv, v * beta[..., None])
    w = jnp.einsum("bhnij,bhnjd->bhnid", t_inv, k_beta * jnp.exp(g_cum)[..., None])
    attn = jnp.einsum("bhnid,bhnjd->bhnij", q, k) * decay
    q_dec = q * jnp.exp(g_cum)[..., None]
    k_dec = k * jnp.exp(g_cum[..., -1:] - g_cum)[..., None]
    g_last = jnp.exp(g_cum[..., -1])

    to_scan = lambda t: jnp.moveaxis(t, 2, 0)
    xs = (to_scan(q_dec), to_scan(k_dec), to_scan(u), to_scan(w), to_scan(attn), to_scan(g_last))

    def step(state, inp):
        q_c, k_c, u_c, w_c, a_c, gl = inp
        v_new = u_c - jnp.einsum("bhid,bhde->bhie", w_c, state)
        o_c = jnp.einsum("bhid,bhde->bhie", q_c, state) + jnp.einsum("bhij,bhje->bhie", a_c, v_new)
        state = state * gl[..., None, None] + jnp.einsum("bhid,bhie->bhde", k_c, v_new)
        return state, o_c

    state0 = jnp.zeros((b_, h, dk, dv), dtype=q.dtype)
    _, o = lax.scan(step, state0, xs)
    return jnp.moveaxis(o, 0, 2).reshape(b_, h, s, dv)


def _sgu_branch(u, v, z, ln_g, ln_b, w_s, b_s):
    b_, s, _ = u.shape
    u = jax.nn.gelu(u, approximate=False)
    v = _layernorm(jax.nn.gelu(v, approximate=False), ln_g, ln_b)
    v = v.reshape(b_, s // SGU_CHUNK, SGU_CHUNK, SGU_GROUPS, SGU_GROUP_DIM)
    sp = jnp.einsum("gij,bcjgd->bcigd", w_s, v) + b_s.T[None, None, :, :, None]
    return u * sp.reshape(b_, s, D_SGU) * jax.nn.silu(z)


def _dn_gates(a, b, a_log, dt_bias):
    g = -jnp.exp(a_log.astype(jnp.float32)) * jax.nn.softplus(
        a.astype(jnp.float32) + dt_bias.astype(jnp.float32))
    beta = jax.nn.sigmoid(b.astype(jnp.float32))
    return g.transpose(0, 2, 1), beta.transpose(0, 2, 1)


def _deltanet_branch(qkv, z, a_f, a_b, b_f, b_b, conv_w, a_log_f, a_log_b,
                     dt_bias_f, dt_bias_b, norm_w):
    b_, s, _ = qkv.shape
    qkv = jax.nn.silu(_short_conv(qkv, conv_w))
    q, k, v = jnp.split(qkv, 3, axis=-1)
    heads = lambda t: t.reshape(b_, s, DN_HEADS, DN_HEAD_DIM).transpose(0, 2, 1, 3).astype(jnp.float32)
    q = _l2norm(heads(q)) * (DN_HEAD_DIM ** -0.5)
    k = _l2norm(heads(k))
    v = heads(v)
    g_f, beta_f = _dn_gates(a_f, b_f, a_log_f, dt_bias_f)
    g_b, beta_b = _dn_gates(a_b, b_b, a_log_b, dt_bias_b)
    rev = lambda t: jnp.flip(t, axis=2)
    o_f = _gated_delta_chunked(q, k, v, g_f, beta_f)
    o_b = rev(_gated_delta_chunked(rev(q), rev(k), rev(v), rev(g_b), rev(beta_b)))
    o = (o_f + o_b).transpose(0, 2, 1, 3)
    o = _rmsnorm(o, norm_w) * jax.nn.silu(z.reshape(b_, s, DN_HEADS, DN_HEAD_DIM).astype(jnp.float32))
    return o.reshape(b_, s, D_DN).astype(z.dtype)


def setup_inputs(seed: int = 0) -> dict:
    key = jax.random.key(seed)
    ks = jax.random.split(key, 16)
    f32 = jnp.float32
    x = jax.random.normal(ks[0], (BATCH, SEQ, D_MODEL), f32)
    norm_w = 1.0 + 0.02 * jax.random.normal(ks[1], (DEPTH, D_MODEL), f32)
    w_in = jax.random.normal(ks[2], (DEPTH, D_MODEL, D_IN), f32) * D_MODEL ** -0.5
    sgu_ln_g = 1.0 + 0.02 * jax.random.normal(ks[3], (DEPTH, D_SGU), f32)
    sgu_ln_b = 0.02 * jax.random.normal(ks[4], (DEPTH, D_SGU), f32)
    sgu_w = jax.random.normal(ks[5], (DEPTH, SGU_GROUPS, SGU_CHUNK, SGU_CHUNK), f32) * SGU_CHUNK ** -0.5
    sgu_b = 1.0 + 0.1 * jax.random.normal(ks[6], (DEPTH, SGU_GROUPS, SGU_CHUNK), f32)
    conv_w = jax.random.normal(ks[7], (DEPTH, CONV_WIDTH, 3 * D_DN), f32) * CONV_WIDTH ** -0.5
    a_log_f = jnp.log(jax.random.uniform(ks[8], (DEPTH, DN_HEADS), f32, 1.0, 16.0))
    a_log_b = jnp.log(jax.random.uniform(ks[9], (DEPTH, DN_HEADS), f32, 1.0, 16.0))
    dt_f = jnp.exp(jax.random.uniform(ks[10], (DEPTH, DN_HEADS), f32, np.log(1e-3), np.log(1e-1)))
    dt_b = jnp.exp(jax.random.uniform(ks[11], (DEPTH, DN_HEADS), f32, np.log(1e-3), np.log(1e-1)))
    dt_bias_f = dt_f + jnp.log(-jnp.expm1(-dt_f))
    dt_bias_b = dt_b + jnp.log(-jnp.expm1(-dt_b))
    dn_norm_w = 1.0 + 0.02 * jax.random.normal(ks[12], (DEPTH, DN_HEAD_DIM), f32)
    w_out = jax.random.normal(ks[13], (DEPTH, D_MIX, D_MODEL), f32) * D_MIX ** -0.5
    final_norm_w = 1.0 + 0.02 * jax.random.normal(ks[14], (D_MODEL,), f32)
    return {"x": x, "norm_w": norm_w, "w_in": w_in, "sgu_ln_g": sgu_ln_g,
            "sgu_ln_b": sgu_ln_b, "sgu_w": sgu_w, "sgu_b": sgu_b, "conv_w": conv_w,
            "a_log_f": a_log_f, "a_log_b": a_log_b, "dt_bias_f": dt_bias_f,
            "dt_bias_b": dt_bias_b, "dn_norm_w": dn_norm_w, "w_out": w_out,
            "final_norm_w": final_norm_w}


def reference(x, norm_w, w_in, sgu_ln_g, sgu_ln_b, sgu_w, sgu_b, conv_w,
              a_log_f, a_log_b, dt_bias_f, dt_bias_b, dn_norm_w, w_out,
              final_norm_w):
    for l in range(DEPTH):
        h = _rmsnorm(x, norm_w[l])
        proj = jnp.einsum("bsd,de->bse", h, w_in[l])
        u, v, z_a, qkv, z_b, a_f, a_b, b_f, b_b = _split_in(proj)
        y_a = _sgu_branch(u, v, z_a, sgu_ln_g[l], sgu_ln_b[l], sgu_w[l], sgu_b[l])
        y_b = _deltanet_branch(qkv, z_b, a_f, a_b, b_f, b_b, conv_w[l], a_log_f[l],
                               a_log_b[l], dt_bias_f[l], dt_bias_b[l], dn_norm_w[l])
        y = jnp.concatenate([y_a, y_b], axis=-1)
        x = x + jnp.einsum("bse,ed->bsd", y, w_out[l])
    return _rmsnorm(x, final_norm_w)
```

```python
import os
import numpy as np
from contextlib import ExitStack
import concourse.bass as bass
import concourse.mybir as mybir
from concourse.bass_utils import run_bass_kernel_spmd

F32 = mybir.dt.float32
F32R = mybir.dt.float32r
BF16 = mybir.dt.bfloat16
AF = mybir.ActivationFunctionType
ALU = mybir.AluOpType
AX = mybir.AxisListType

D = 2048
L = 4
T = 2048
NT = 16
SEQ = 8192
KC = 16
EPS = 1e-6
NTS = SEQ // 128
RT = BF16
RT2 = BF16 if os.environ.get('MK_RT2', 'f32r') == 'bf16' else F32R
BIG = 30000.0
DBG = int(os.environ.get("MK_DBG", 0))
STOP = int(os.environ.get("MK_STOP", 9))

C_ID, C_ONE, C_TRF, C_TRB, C_N1F, C_N1B, C_P2F, C_P2B = range(8)


class _Rec:
    def __getattr__(self, name):
        def f(*a, **k):
            self.call = (name, a, k)
            return self
        return f


class Sched:
    def __init__(self, nc, stack):
        self.nc = nc
        self.stack = stack
        self.names = ['pe', 'dve', 'act', 'pool', 'sp']
        self.ops = {k: [] for k in self.names}
        self.epoch = 0
        self.psem = {k: stack.enter_context(nc.semaphore("p_" + k + "_0")) for k in self.names}
        self.pkey = {k: "p_%s_0" % k for k in self.names}
        self.cnt = {k: 0 for k in self.names}
        self.waited = {k: {} for k in self.names}
        self.state = {}
        self.dsem = {}

    def new_epoch(self):
        self.epoch += 1
        for k in self.names:
            self.psem[k] = self.stack.enter_context(self.nc.semaphore("p_%s_%d" % (k, self.epoch)))
            self.pkey[k] = "p_%s_%d" % (k, self.epoch)
            self.cnt[k] = 0

    def _slot(self, slot):
        if slot not in self.dsem:
            self.dsem[slot] = [self.stack.enter_context(self.nc.semaphore("d_" + str(slot))), 0]
        return self.dsem[slot]

    def op(self, eng, fn, r=(), w=(), dma=None, cc=None):
        r2, w2 = [], []
        for b in r:
            if isinstance(b, tuple) and b[0] == 'pb':
                if ('pb', b[1]) not in w2:
                    w2.append(('pb', b[1]))
            else:
                r2.append(b)
        for b in w:
            if isinstance(b, tuple) and b[0] == 'pb':
                b = ('pb', b[1])
            if b not in w2:
                w2.append(b)
        r, w = r2, w2
        need = {}

        def add(tok):
            if tok is not None and need.get(tok[0], (None, -1))[1] < tok[1]:
                need[tok[0]] = tok
        for b in r:
            st = self.state.get(b)
            if st:
                add(st[0])
        for b in w:
            st = self.state.get(b)
            if st:
                add(st[0])
                for t in st[1].values():
                    add(t)
        waits = []
        for k, tok in need.items():
            if eng == 'pe' and k.startswith('p_pe_'):
                continue
            if self.waited[eng].get(k, -1) >= tok[1]:
                continue
            self.waited[eng][k] = tok[1]
            waits.append((tok[2], tok[1]))
        if dma is not None:
            sl = self._slot(dma)
            sl[1] += 16
            tok = ("d_" + str(dma), sl[1], sl[0])
            inc = (sl[0], 16)
        elif cc is not None:
            sl = self._slot(cc)
            sl[1] += 1
            tok = ("d_" + str(cc), sl[1], sl[0])
            inc = (sl[0], None)
        else:
            self.cnt[eng] += 1
            tok = (self.pkey[eng], self.cnt[eng], self.psem[eng])
            inc = (self.psem[eng], 1)
        rec = _Rec()
        fn(rec)
        self.ops[eng].append((waits, rec.call, inc))
        for b in w:
            self.state[b] = [tok, {}]
        for b in r:
            if b in w:
                continue
            st = self.state.setdefault(b, [None, {}])
            st[1][tok[0]] = tok
        return tok

    def barrier(self):
        toks = []
        for k in self.names:
            if self.cnt[k] > 0:
                toks.append((self.pkey[k], self.cnt[k], self.psem[k]))
        for slot, (sem, cum) in self.dsem.items():
            if cum > 0:
                toks.append(("d_" + str(slot), cum, sem))
        for eng in self.names:
            waits = []
            for tok in toks:
                if tok[0] == self.pkey[eng]:
                    continue
                if self.waited[eng].get(tok[0], -1) >= tok[1]:
                    continue
                self.waited[eng][tok[0]] = tok[1]
                waits.append((tok[2], tok[1]))
            if waits:
                self.ops[eng].append((waits, None, None))

    def emit(self):
        nc = self.nc
        ops = self.ops

        def run(k, e):
            for waits, fn, inc in ops[k]:
                for sem, val in waits:
                    e.wait_ge(sem, val)
                if fn is None:
                    continue
                ins = getattr(e, fn[0])(*fn[1], **fn[2])
                if inc[1] is None:
                    ins.then_inc(inc[0])
                else:
                    ins.then_inc(inc[0], inc[1])
        with nc.Block() as block:
            @block.sync
            def _(e):
                run('sp', e)

            @block.tensor
            def _(e):
                run('pe', e)

            @block.vector
            def _(e):
                run('dve', e)

            @block.scalar
            def _(e):
                run('act', e)

            @block.gpsimd
            def _(e):
                run('pool', e)


def build(depth=L, test=None):
    nc = bass.Bass("TRN2", target_bir_lowering=False)
    stack = ExitStack()
    S = Sched(nc, stack)

    def din(name, shape, dt=F32):
        return nc.dram_tensor(name, list(shape), dt, kind="ExternalInput").ap()

    if test is None:
        x_in = din("x", [T, D])
        normw = din("normw", [L, 128, KC])
        wsg = din("wsg", [L, 32, 128, KC, 128])
        wdn = din("wdn", [L, 6, 128, KC, 128])
        wdg = din("wdg", [L, 128, KC, 8])
        wout = din("wout", [L, 4, 128, KC, 512])
        lng = din("lng", [L, 128, 1024])
        lnb = din("lnb", [L, 128, 1024])
        sguwT = din("sguwT", [L, 128, 8, 128])
        sgub = din("sgub", [L, 128, 8, 128])
        convw = din("convw", [L, 128, 6, 5])
        alog = din("alog", [L, 128, 4])
        dtb = din("dtb", [L, 128, 4])
        dnw = din("dnw", [128, L])
        fnw = din("fnw", [128, D])
        sel = din("sel", [128, 4])
        consts = din("consts", [128, 8, 128])
        out = nc.dram_tensor("out", [T, D], F32, kind="ExternalOutput").ap()

        x_res = nc.dram_tensor("x_res", [T, D], F32).ap()
        hs = [nc.dram_tensor("hs%d" % i, [256, T], BF16).ap() for i in range(8)]
        ha = [nc.dram_tensor("ha%d" % i, [4 * 256, T], BF16).ap() for i in range(8)]
        qkv_s = nc.dram_tensor("qkv_s", [6 * 128, SEQ], F32).ap()
        os_ = [nc.dram_tensor("os%d" % i, [1024, 256], F32).ap() for i in range(8)]
        oa = [nc.dram_tensor("oa%d" % i, [4 * 1024, 256], F32).ap() for i in range(8)]

    else:
        consts = din("consts", [128, 8, 128])
        qkv_in = din("qkv_in", [6 * 128, test * 128])
        gates_in = din("gates_in", [128, test, 8])
        o_out = nc.dram_tensor("o_out", [test * 128, 256], F32, kind="ExternalOutput").ap()
    RG = [[0, 1, 2, 3], [4, 5, 6, 7]]
    if DBG and test is None:
        dbg_qkv = nc.dram_tensor("dbg_qkv", [6 * 128, SEQ], F32, kind="ExternalOutput").ap()
        dbg_g = nc.dram_tensor("dbg_g", [128, NTS, 8], F32, kind="ExternalOutput").ap()
        dbg_o = nc.dram_tensor("dbg_o", [SEQ, 256], F32, kind="ExternalOutput").ap()
        dbg_ya = nc.dram_tensor("dbg_ya", [128, 8, T], BF16, kind="ExternalOutput").ap()
        dbg_yb = nc.dram_tensor("dbg_yb", [128, 8, T], BF16, kind="ExternalOutput").ap()
        dbg_vn = nc.dram_tensor("dbg_vn", [128, NT, 1024], BF16, kind="ExternalOutput").ap()

    uid = [0]

    def sb(name, shape, dt=F32, st=None):
        uid[0] += 1
        return (st or stack).enter_context(nc.sbuf_tensor("%s_%d" % (name, uid[0]), list(shape), dt))

    pb = [stack.enter_context(nc.psum_tensor("pb%d" % i, [128, 512], F32)) for i in range(8)]

    cst = sb("cst", [128, 8, 128])
    identb = sb("identb", [128, 128], BF16)
    S.op('sp', lambda e: e.dma_start(out=cst[:], in_=consts), w=['cst'], dma='c0')
    if test is None:
        selt = sb("selt", [128, 4])
        dnwt = sb("dnwt", [128, L])
        xT = sb("xT", [128, KC, T], BF16)
        ALLXT = [('xT', t) for t in range(NT)]
        S.op('sp', lambda e: e.dma_start(out=selt[:], in_=sel), w=['selt'], dma='c1')
        S.op('sp', lambda e: e.dma_start(out=dnwt[:], in_=dnw), w=['dnwt'], dma='c2')
    S.op('dve', lambda e: e.tensor_copy(out=identb[:], in_=cst[:, C_ID, :]), r=['cst'], w=['identb'])
    IDENT = cst[:, C_ID, :]
    ONES = cst[:, C_ONE, :]

    def rsqrt_chain(src_ap, dst_ap, scale, keys_r, key_w, tmp_ap, tmpkey):
        S.op('dve', lambda e: e.tensor_scalar(out=tmp_ap, in0=src_ap, scalar1=scale, scalar2=EPS,
                                              op0=ALU.mult, op1=ALU.add), r=keys_r, w=[tmpkey])
        S.op('act', lambda e: e.activation(out=tmp_ap, in_=tmp_ap, func=AF.Sqrt), r=[tmpkey], w=[tmpkey])
        S.op('dve', lambda e: e.reciprocal(out=dst_ap, in_=tmp_ap), r=[tmpkey], w=[key_w])

    def run_d2(gates, qkv_s, os_, nts):
        npairs = nts // 2
        with ExitStack() as st:
            def bc(ap2, n=128):
                return ap2.unsqueeze(2).broadcast_to([128, ap2.shape[1], n])
            IDENT4 = IDENT.unsqueeze(1).broadcast_to([128, 4, 128])

            def rd(ap):
                return ap.bitcast(F32) if RT2 == F32R else ap
            qview = qkv_s.rearrange("(h c p) t -> p h c t", h=2, c=3)

            def dir_gen(d):
                pre = "c%d_" % d
                BA, BB, BC, BD = 4 * d, 4 * d + 1, 4 * d + 2, 4 * d + 3
                TRI = cst[:, C_TRF + d, :]
                NEG1_4 = cst[:, C_N1F + d, :].unsqueeze(1).broadcast_to([128, 4, 128])
                POS2_4 = cst[:, C_P2F + d, :].unsqueeze(1).broadcast_to([128, 4, 128])
                K = lambda n: pre + n

                def t4(name, dt=F32):
                    return sb(pre + name, [128, 4, 128], dt, st)

                def t2(name, dt=F32):
                    return sb(pre + name, [128, 2, 128], dt, st)
                qT = [t4("qT0"), t4("qT1")]
                kT = [t4("kT0"), t4("kT1")]
                vT = [t4("vT0"), t4("vT1")]
                kTr, ktok, vb = t4("kTr", RT2), t4("ktok"), t4("vb", RT2)
                A1, A2, scr = t4("A1"), t4("A2"), t4("scr")
                Pm = [t4("P0", RT2), t4("P1", RT2)]
                PTm = [t4("PT0", RT2), t4("PT1", RT2)]
                Y, kbg = t4("Y", RT2), t4("kbg", RT2)
                gs = sb(pre + "gs", [128, 2, 36], F32, st)
                HqT = sb(pre + "HqT", [128, 4, 2, 128], RT2, st)
                HwT = sb(pre + "HwT", [128, 4, 2, 128], RT2, st)
                Hu = sb(pre + "Hu", [128, 4, 2, 128], F32, st)
                Hkd = sb(pre + "Hkd", [128, 4, 2, 128], RT2, st)
                Hat = sb(pre + "Hat", [128, 4, 2, 128], RT2, st)
                Hgs = sb(pre + "Hgs", [128, 4, 2, 2], F32, st)
                vnew, oqs = t2("vnew", RT2), t2("oqs")
                ot = [t2("ot0"), t2("ot1")]
                oprev, St = t2("oprev"), t2("S", RT2)
                Sm = t2("Sm") if RT2 == BF16 else None
                for h in range(2):
                    S.op('dve', lambda e, h=h: e.tensor_scalar(out=St[:, h, :], in0=IDENT, scalar1=0.0, scalar2=None,
                                                               op0=ALU.mult), r=['cst'], w=[K('S')])
                    if Sm is not None:
                        S.op('dve', lambda e, h=h: e.tensor_scalar(out=Sm[:, h, :], in0=IDENT, scalar1=0.0,
                                                                   scalar2=None, op0=ALU.mult), r=['cst'], w=[K('Sm')])

                def tile_of(step):
                    return step if d == 0 else nts - 1 - step

                def h4(H, s0):
                    return H[:, s0:s0 + 2].rearrange("p a h c -> p (a h) c")

                def hk(nm, s0):
                    return [K('%s%d' % (nm, s0)), K('%s%d' % (nm, s0 + 1))]

                def load(p):
                    pp = p % 2
                    for tt in range(2):
                        n = tile_of(2 * p + tt)
                        for (buf, ty, nm) in ((qT, 0, 'qT'), (kT, 1, 'kT'), (vT, 2, 'vT')):
                            src = qview[:, :, ty, n * 128:(n + 1) * 128]
                            S.op('sp', lambda e, buf=buf, src=src, tt=tt: e.dma_start(
                                out=buf[pp][:, tt * 2:tt * 2 + 2, :], in_=src),
                                r=[('qkv', hh * 3 + ty, n // 4) for hh in range(2)], w=[K(nm + str(pp))],
                                dma=pre + nm + str(pp))

                def prep(p):
                    pp = p % 2
                    s0 = (2 * p) % 4
                    g = gs[:, pp, :]
                    GK = K('gs%d' % pp)
                    qTi, kTi, vTi = qT[pp], kT[pp], vT[pp]
                    S.op('pool', lambda e: e.tensor_copy(out=h4(HqT, s0), in_=qTi[:]), r=[K('qT%d' % pp)],
                         w=hk('HqT', s0))
                    S.op('pool', lambda e: e.tensor_copy(out=kTr[:], in_=kTi[:]), r=[K('kT%d' % pp)], w=[K('kTr')])
                    yield
                    trk = pb[BA][:].rearrange("p (a c) -> p a c", a=4)
                    trv = pb[BB][:].rearrange("p (a c) -> p a c", a=4)
                    for j in range(4):
                        S.op('pe', lambda e, j=j: e.transpose(out=trk[:, j, :], in_=kTi[:, j, :], identity=IDENT),
                             r=[K('kT%d' % pp), 'cst'], w=[('pb', BA)])
                    for j in range(4):
                        S.op('pe', lambda e, j=j: e.transpose(out=trv[:, j, :], in_=vTi[:, j, :], identity=IDENT),
                             r=[K('vT%d' % pp), 'cst'], w=[('pb', BB)])
                    yield
                    for tt in range(2):
                        n = tile_of(2 * p + tt)
                        gv = gates[:, n, :].rearrange("p (x h e) -> p x h e", x=2, h=2)
                        S.op('dve', lambda e, gv=gv, tt=tt: e.tensor_copy(out=g[:, tt * 2:tt * 2 + 2], in_=gv[:, 0, :, d]),
                             r=['gates'], w=[GK])
                        S.op('dve', lambda e, gv=gv, tt=tt: e.tensor_copy(out=g[:, 32 + tt * 2:34 + tt * 2],
                                                                          in_=gv[:, 1, :, d]), r=['gates'], w=[GK])
                    gp = pb[BC][:, 0:8]
                    S.op('pe', lambda e: e.matmul(gp[:, 0:4], lhsT=TRI, rhs=g[:, 0:4], start=True, stop=True),
                         r=['cst', GK], w=[('pb', BC)])
                    S.op('pe', lambda e: e.matmul(gp[:, 4:8], lhsT=ONES, rhs=g[:, 0:4], start=True, stop=True),
                         r=['cst', GK], w=[('pb', BC)])
                    S.op('act', lambda e: e.copy(out=g[:, 4:12], in_=gp), r=[('pb', BC)], w=[GK])
                    S.op('act', lambda e: e.activation(out=g[:, 12:16], in_=g[:, 4:8], func=AF.Exp), r=[GK], w=[GK])
                    S.op('dve', lambda e: e.tensor_tensor(out=g[:, 16:20], in0=g[:, 8:12], in1=g[:, 4:8],
                                                          op=ALU.subtract), r=[GK], w=[GK])
                    S.op('act', lambda e: e.activation(out=g[:, 16:20], in_=g[:, 16:20], func=AF.Exp), r=[GK], w=[GK])
                    S.op('act', lambda e: e.activation(out=g[:, 20:24], in_=g[:, 8:12], func=AF.Exp), r=[GK], w=[GK])
                    S.op('dve', lambda e: e.tensor_tensor(out=g[:, 24:28], in0=g[:, 12:16], in1=g[:, 32:36],
                                                          op=ALU.mult), r=[GK], w=[GK])
                    S.op('dve', lambda e: e.tensor_scalar(out=g[:, 28:32], in0=g[:, 32:36], scalar1=-1.0,
                                                          scalar2=None, op0=ALU.mult), r=[GK], w=[GK])
                    S.op('pool', lambda e: e.tensor_copy(out=Hgs[:, s0:s0 + 2, 0, :],
                                                         in_=g[:, 12:16].rearrange("p (a h) -> p a h", a=2)),
                         r=[GK], w=hk('Hgs', s0))
                    S.op('pool', lambda e: e.tensor_copy(out=Hgs[:, s0:s0 + 2, 1, :],
                                                         in_=g[:, 20:24].rearrange("p (a h) -> p a h", a=2)),
                         r=[GK], w=hk('Hgs', s0))
                    yield
                    S.op('act', lambda e: e.copy(out=ktok[:], in_=trk), r=[('pb', BA)], w=[K('ktok')])
                    S.op('dve', lambda e: e.tensor_tensor(out=vb[:], in0=trv, in1=bc(g[:, 32:36]), op=ALU.mult),
                         r=[('pb', BB), GK], w=[K('vb')])
                    yield
                    bn = pb[BA][:].rearrange("p (a c) -> p a c", a=4)
                    S.op('dve', lambda e: e.tensor_tensor(out=scr[:], in0=IDENT4, in1=bc(g[:, 4:8]), op=ALU.mult),
                         r=['cst', GK], w=[K('scr')])
                    for j in range(4):
                        S.op('pe', lambda e, j=j: e.matmul(bn[:, j, :], lhsT=ONES, rhs=scr[:, j, :], start=True,
                                                           stop=True), r=['cst', K('scr')], w=[('pb', BA)])
                    yield
                    S.op('dve', lambda e: e.tensor_tensor(out=A1[:], in0=bn, in1=bc(g[:, 4:8]), op=ALU.subtract),
                         r=[('pb', BA), GK], w=[K('A1')])
                    S.op('dve', lambda e: e.tensor_tensor(out=A2[:], in0=A1[:], in1=POS2_4, op=ALU.add),
                         r=[K('A1'), 'cst'], w=[K('A2')])
                    S.op('dve', lambda e: e.tensor_tensor(out=A1[:], in0=A1[:], in1=NEG1_4, op=ALU.add),
                         r=[K('A1'), 'cst'], w=[K('A1')])
                    S.op('act', lambda e: e.activation(out=A1[:], in_=A1[:], func=AF.Exp), r=[K('A1')], w=[K('A1')])
                    S.op('act', lambda e: e.activation(out=A2[:], in_=A2[:], func=AF.Exp, scale=-1.0),
                         r=[K('A2')], w=[K('A2')])
                    yield
                    kk = pb[BB][:].rearrange("p (a c) -> p a c", a=4)
                    zt = pb[BC][:].rearrange("p (a c) -> p a c", a=4)
                    for j in range(4):
                        S.op('pe', lambda e, j=j: e.matmul(kk[:, j, :], lhsT=kTr[:, j, :], rhs=kTr[:, j, :],
                                                           start=True, stop=True), r=[K('kTr')], w=[('pb', BB)])
                    for j in range(4):
                        S.op('pe', lambda e, j=j: e.matmul(zt[:, j, :], lhsT=kTr[:, j, :],
                                                           rhs=HqT[:, s0 + j // 2, j % 2, :], start=True, stop=True),
                             r=[K('kTr')] + hk('HqT', s0), w=[('pb', BC)])
                    yield
                    S.op('dve', lambda e: e.tensor_tensor(out=scr[:], in0=kk, in1=bc(g[:, 28:32]), op=ALU.mult),
                         r=[('pb', BB), GK], w=[K('scr')])
                    S.op('dve', lambda e: e.tensor_tensor(out=scr[:], in0=scr[:], in1=A2[:], op=ALU.mult),
                         r=[K('scr'), K('A2')], w=[K('scr')])
                    S.op('dve', lambda e: e.tensor_tensor(out=h4(Hat, s0), in0=zt, in1=A1[:], op=ALU.mult),
                         r=[('pb', BC), K('A1')], w=hk('Hat', s0))
                    S.op('act', lambda e: e.copy(out=Pm[0][:], in_=scr[:]), r=[K('scr')], w=[K('P0')])
                    yield
                    nt = pb[BA][:].rearrange("p (a c) -> p a c", a=4)
                    for j in range(4):
                        S.op('pe', lambda e, j=j: e.transpose(out=nt[:, j, :], in_=scr[:, j, :], identity=IDENT),
                             r=[K('scr'), 'cst'], w=[('pb', BA)])
                    S.op('act', lambda e: e.copy(out=PTm[0][:], in_=nt), r=[('pb', BA)], w=[K('PT0')])
                    S.op('dve', lambda e: e.tensor_tensor(out=Y[:], in0=nt, in1=IDENT4, op=ALU.add),
                         r=[('pb', BA), 'cst'], w=[K('Y')])
                    yield
                    for k in range(6):
                        a, b = k % 2, (k + 1) % 2
                        p2 = pb[BB][:].rearrange("p (a c) -> p a c", a=4)
                        pt2 = pb[BC][:].rearrange("p (a c) -> p a c", a=4)
                        yu = pb[BA][:].rearrange("p (a c) -> p a c", a=4)
                        for j in range(4):
                            S.op('pe', lambda e, j=j, a=a: e.matmul(p2[:, j, :], lhsT=PTm[a][:, j, :],
                                                                    rhs=Pm[a][:, j, :], start=True, stop=True),
                                 r=[K('PT%d' % a), K('P%d' % a)], w=[('pb', BB)])
                        if k < 5:
                            for j in range(4):
                                S.op('pe', lambda e, j=j, a=a: e.matmul(pt2[:, j, :], lhsT=Pm[a][:, j, :],
                                                                        rhs=PTm[a][:, j, :], start=True, stop=True),
                                     r=[K('PT%d' % a), K('P%d' % a)], w=[('pb', BC)])
                        S.op('act', lambda e, b=b: e.copy(out=Pm[b][:], in_=p2), r=[('pb', BB)], w=[K('P%d' % b)])
                        if k < 5:
                            S.op('dve', lambda e, b=b: e.tensor_copy(out=PTm[b][:], in_=pt2), r=[('pb', BC)],
                                 w=[K('PT%d' % b)])
                        yield
                        for j in range(4):
                            S.op('pe', lambda e, j=j, b=b: e.matmul(yu[:, j, :], lhsT=Pm[b][:, j, :], rhs=Y[:, j, :],
                                                                    start=True, stop=True),
                                 r=[K('P%d' % b), K('Y')], w=[('pb', BA)])
                        S.op('dve', lambda e: e.tensor_tensor(out=Y[:], in0=rd(Y[:]), in1=yu, op=ALU.add),
                             r=[K('Y'), ('pb', BA)], w=[K('Y')])
                        yield
                    up = pb[BB][:].rearrange("p (a c) -> p a c", a=4)
                    wp = pb[BC][:].rearrange("p (a c) -> p a c", a=4)
                    S.op('dve', lambda e: e.tensor_tensor(out=kbg[:], in0=ktok[:], in1=bc(g[:, 24:28]), op=ALU.mult),
                         r=[K('ktok'), GK], w=[K('kbg')])
                    S.op('pool', lambda e: e.tensor_tensor(out=h4(Hkd, s0), in0=ktok[:], in1=bc(g[:, 16:20]),
                                                           op=ALU.mult), r=[K('ktok'), GK], w=hk('Hkd', s0))
                    for j in range(4):
                        S.op('pe', lambda e, j=j: e.matmul(up[:, j, :], lhsT=Y[:, j, :], rhs=vb[:, j, :], start=True,
                                                           stop=True), r=[K('Y'), K('vb')], w=[('pb', BB)])
                    for j in range(4):
                        S.op('pe', lambda e, j=j: e.matmul(wp[:, j, :], lhsT=kbg[:, j, :], rhs=Y[:, j, :], start=True,
                                                           stop=True), r=[K('Y'), K('kbg')], w=[('pb', BC)])
                    S.op('act', lambda e: e.copy(out=h4(Hu, s0), in_=up), r=[('pb', BB)], w=hk('Hu', s0))
                    S.op('act', lambda e: e.copy(out=h4(HwT, s0), in_=wp), r=[('pb', BC)], w=hk('HwT', s0))
                    yield

                def seq(step):
                    n = tile_of(step)
                    s = step % 4
                    i = step % 2
                    second = step >= nts // 2
                    odst = os_[n // 8][(n % 8) * 128:(n % 8 + 1) * 128, :].rearrange("p (h c) -> p h c", h=2)
                    if second:
                        S.op('sp', lambda e: e.dma_start(out=oprev[:], in_=odst), r=[('osend', n)], w=[K('oprev')],
                             dma=pre + 'oprev')
                    q4 = pb[BD][:].rearrange("p (a c) -> p a c", a=4)
                    for h in range(2):
                        S.op('pe', lambda e, h=h: e.matmul(q4[:, h, :], lhsT=HwT[:, s, h, :], rhs=St[:, h, :],
                                                           start=True, stop=True),
                             r=[K('HwT%d' % s), K('S')], w=[('pb', BD)])
                        S.op('pe', lambda e, h=h: e.matmul(q4[:, 2 + h, :], lhsT=HqT[:, s, h, :], rhs=St[:, h, :],
                                                           start=True, stop=True),
                             r=[K('HqT%d' % s), K('S')], w=[('pb', BD)])
                    S.op('dve', lambda e: e.tensor_tensor(out=vnew[:], in0=Hu[:, s], in1=q4[:, 0:2, :],
                                                          op=ALU.subtract),
                         r=[K('Hu%d' % s), ('pb', BD)], w=[K('vnew')])
                    S.op('dve', lambda e: e.tensor_tensor(out=oqs[:], in0=q4[:, 2:4, :], in1=bc(Hgs[:, s, 0, :]),
                                                          op=ALU.mult),
                         r=[('pb', BD), K('Hgs%d' % s)], w=[K('oqs')])
                    yield
                    for h in range(2):
                        S.op('pe', lambda e, h=h: e.matmul(q4[:, h, :], lhsT=Hat[:, s, h, :], rhs=vnew[:, h, :],
                                                           start=True, stop=True),
                             r=[K('Hat%d' % s), K('vnew')], w=[('pb', BD)])
                        S.op('pe', lambda e, h=h: e.matmul(q4[:, 2 + h, :], lhsT=Hkd[:, s, h, :], rhs=vnew[:, h, :],
                                                           start=True, stop=True),
                             r=[K('Hkd%d' % s), K('vnew')], w=[('pb', BD)])
                    oti = ot[i]
                    S.op('dve', lambda e: e.tensor_tensor(out=oti[:], in0=oqs[:], in1=q4[:, 0:2, :], op=ALU.add),
                         r=[K('oqs'), ('pb', BD)], w=[K('ot%d' % i)])
                    if Sm is None:
                        for h in range(2):
                            S.op('dve', lambda e, h=h: e.scalar_tensor_tensor(
                                out=St[:, h, :], in0=St[:, h, :].bitcast(F32), scalar=Hgs[:, s, 1, h:h + 1],
                                in1=q4[:, 2 + h, :], op0=ALU.mult, op1=ALU.add),
                                r=[K('S'), K('Hgs%d' % s), ('pb', BD)], w=[K('S')])
                    else:
                        for h in range(2):
                            S.op('dve', lambda e, h=h: e.scalar_tensor_tensor(
                                out=Sm[:, h, :], in0=Sm[:, h, :], scalar=Hgs[:, s, 1, h:h + 1],
                                in1=q4[:, 2 + h, :], op0=ALU.mult, op1=ALU.add),
                                r=[K('Sm'), K('Hgs%d' % s), ('pb', BD)], w=[K('Sm')])
                        S.op('act', lambda e: e.copy(out=St[:], in_=Sm[:]), r=[K('Sm')], w=[K('S')])
                    if second:
                        S.op('pool', lambda e: e.tensor_tensor(out=oti[:], in0=oti[:], in1=oprev[:], op=ALU.add),
                             r=[K('ot%d' % i), K('oprev')], w=[K('ot%d' % i)])
                    S.op('sp', lambda e: e.dma_start(out=odst, in_=oti[:]), r=[K('ot%d' % i)], w=[('osend', n)],
                         dma=pre + 'ot%d' % i)
                    yield

                def chain2(*gs_):
                    for g_ in gs_:
                        for _ in g_:
                            yield

                load(0)
                if npairs > 1:
                    load(1)
                yield
                for _ in prep(0):
                    yield
                for p in range(npairs):
                    if p + 2 < npairs:
                        load(p + 2)
                    A = prep(p + 1) if p + 1 < npairs else iter(())
                    B = chain2(seq(2 * p), seq(2 * p + 1))
                    a_alive, b_alive, cnt = True, True, 0
                    while a_alive or b_alive:
                        cnt += 1
                        if a_alive:
                            try:
                                next(A)
                            except StopIteration:
                                a_alive = False
                            yield
                        if b_alive and (cnt % 5 == 0 or not a_alive):
                            try:
                                next(B)
                            except StopIteration:
                                b_alive = False
                            yield

            gens = [dir_gen(0), dir_gen(1)]
            alive = [True, True]
            while any(alive):
                for gi, g in enumerate(gens):
                    if alive[gi]:
                        try:
                            next(g)
                        except StopIteration:
                            alive[gi] = False

    if test is not None:
        gates_t = sb("gates_t", [128, test, 8], F32)
        S.op('sp', lambda e: e.dma_start(out=gates_t[:], in_=gates_in), w=['gates'], dma='c1')
        os_t = [o_out[i * 1024:(i + 1) * 1024, :] for i in range(max(1, test // 8))]
        run_d2(gates_t, qkv_in, os_t, test)
        S.barrier()
        S.emit()
        stack.close()
        return nc

    for l in range(depth):
        x_src = x_in if l == 0 else x_res
        with ExitStack() as st:
            xin = [sb("xin%d" % i, [128, D], F32, st) for i in range(2)]
            xbf = [sb("xbf%d" % i, [128, D], BF16, st) for i in range(2)]
            junk = sb("junk", [128, D], BF16, st)
            ssq = sb("ssq", [128, NT], F32, st)
            rtmp = sb("rtmp", [128, NT], F32, st)
            rstd = sb("rstd", [128, NT], F32, st)
            for t in range(NT):
                i = t % 2
                S.op('sp', lambda e, t=t, i=i: e.dma_start(out=xin[i][:], in_=x_src[t * 128:(t + 1) * 128, :]),
                     r=[('xres', t)], w=[('xin', i)], dma='xin%d' % i)
                S.op('act', lambda e, t=t, i=i: e.activation(out=junk[:], in_=xin[i][:], func=AF.Square,
                                                             accum_out=ssq[:, t:t + 1]),
                     r=[('xin', i)], w=['junk', ('ssq', t)])
                rsqrt_chain(ssq[:, t:t + 1], rstd[:, t:t + 1], 1.0 / D, [('ssq', t)], ('rstd', t),
                            rtmp[:, t:t + 1], ('rtmp', t))
                S.op('dve', lambda e, t=t, i=i: e.tensor_scalar(out=xbf[i][:], in0=xin[i][:], scalar1=rstd[:, t:t + 1],
                                                                scalar2=None, op0=ALU.mult),
                     r=[('xin', i), ('rstd', t)], w=[('xbf', i)])
                banks = (0, 1) if i == 0 else (2, 3)
                for kc in range(KC):
                    bk = banks[kc // 8]
                    pv = pb[bk][:].bitcast(BF16)
                    S.op('pe', lambda e, kc=kc, i=i, pv=pv: e.transpose(
                        out=pv[:, (kc % 8) * 128:(kc % 8 + 1) * 128], in_=xbf[i][:, kc * 128:(kc + 1) * 128],
                        identity=identb[:]),
                        r=[('xbf', i), 'identb'], w=[('pb', bk)])
                for hb, eng in ((0, 'act'), (1, 'dve')):
                    bk = banks[hb]
                    pv = pb[bk][:].bitcast(BF16).rearrange("p (k t) -> p k t", k=8)
                    dst = xT[:, hb * 8:(hb + 1) * 8, t * 128:(t + 1) * 128]
                    if eng == 'act':
                        S.op('act', lambda e, pv=pv, dst=dst: e.copy(out=dst, in_=pv), r=[('pb', bk)], w=[('xT', t)])
                    else:
                        S.op('dve', lambda e, pv=pv, dst=dst: e.tensor_copy(out=dst, in_=pv), r=[('pb', bk)],
                             w=[('xT', t)])
                if t % 4 == 3:
                    tb = t // 4
                    for ci in range(8):
                        S.op('sp', lambda e, tb=tb, ci=ci: e.dma_start(
                            out=hs[ci].rearrange("(k p) t -> p k t", p=128)[:, :, tb * 512:(tb + 1) * 512],
                            in_=xT[:, 2 * ci:2 * ci + 2, tb * 512:(tb + 1) * 512]),
                            r=[('xT', tt) for tt in range(tb * 4, tb * 4 + 4)], w=[('hs', ci)], dma='hs')
        if STOP < 2:
            continue
        S.barrier()
        for ci in range(8):
            S.op('pool', lambda e, ci=ci: e.collective_compute("AllGather", ALU.bypass, replica_groups=RG,
                                                               ins=[hs[ci].opt()], outs=[ha[ci].opt()]),
                 r=[('hs', ci)], w=[('ha', ci)], cc='ag1')
        if STOP < 3:
            continue

        with ExitStack() as st2:
            gates = sb("gates", [128, NTS, 8], F32, st2)
            with ExitStack() as st:
                wst = [sb("wst%d" % i, [128, KC, 128], F32, st) for i in range(2)]
                wdnb = sb("wdnb", [128, 6, KC, 128], BF16, st)
                wgs = sb("wgs", [128, KC, 8], F32, st)
                wgb = sb("wgb", [128, KC, 8], BF16, st)
                nwt = sb("nwt", [128, KC], F32, st)
                cwt = sb("cwt", [128, 6, 5], F32, st)
                dg = sb("dg", [128, 30, 128], RT, st)
                onesr = sb("onesr", [128, 128], RT, st)
                alt = sb("alt", [128, 4], F32, st)
                dtt = sb("dtt", [128, 4], F32, st)
                nega = sb("nega", [128, 4], F32, st)
                hblk = [sb("hblk%d" % i, [128, KC, 512], BF16, st) for i in range(2)]
                PB = [sb("PB%d" % i, [128, 6, 516], RT, st) for i in range(2)]
                cv = [sb("cv%d" % i, [128, 512], F32, st) for i in range(2)]
                sq = [sb("sq%d" % i, [128, 512], RT, st) for i in range(2)]
                rn = [sb("rn%d" % i, [128, 512], F32, st) for i in range(2)]
                qo = [sb("qo%d" % i, [128, 512], F32, st) for i in range(2)]
                gt = sb("gt", [128, 4, 8], F32, st)
                S.op('sp', lambda e: e.dma_start(out=nwt[:], in_=normw[l]), w=['nwt'], dma='c0')
                S.op('sp', lambda e: e.dma_start(out=cwt[:], in_=convw[l]), w=['cwt'], dma='c1')
                S.op('sp', lambda e: e.dma_start(out=alt[:], in_=alog[l]), w=['alt'], dma='c2')
                S.op('sp', lambda e: e.dma_start(out=dtt[:], in_=dtb[l]), w=['dtt'], dma='c3')
                S.op('sp', lambda e: e.dma_start(out=wgs[:], in_=wdg[l]), w=['wgs'], dma='c4')
                for c in range(6):
                    i = c % 2
                    S.op('sp', lambda e, c=c, i=i: e.dma_start(out=wst[i][:], in_=wdn[l, c]), w=[('wst', i)],
                         dma='wst%d' % i)
                    S.op('pool', lambda e, c=c, i=i: e.tensor_tensor(
                        out=wdnb[:, c], in0=wst[i][:], in1=nwt[:].unsqueeze(2).broadcast_to([128, KC, 128]),
                        op=ALU.mult), r=[('wst', i), 'nwt'], w=[('wdnb', c)])
                S.op('pool', lambda e: e.tensor_tensor(out=wgb[:], in0=wgs[:],
                                                       in1=nwt[:].unsqueeze(2).broadcast_to([128, KC, 8]),
                                                       op=ALU.mult), r=['wgs', 'nwt'], w=['wgb'])
                for c in range(6):
                    for tau in range(5):
                        S.op('dve', lambda e, c=c, tau=tau: e.tensor_scalar(
                            out=dg[:, c * 5 + tau, :], in0=IDENT, scalar1=cwt[:, c, tau:tau + 1], scalar2=None,
                            op0=ALU.mult), r=['cst', 'cwt'], w=[('dg', c)])
                S.op('dve', lambda e: e.tensor_copy(out=onesr[:], in_=ONES), r=['cst'], w=['onesr'])
                S.op('act', lambda e: e.activation(out=nega[:], in_=alt[:], func=AF.Exp), r=['alt'], w=['nega'])
                S.op('dve', lambda e: e.tensor_scalar(out=nega[:], in0=nega[:], scalar1=-1.0, scalar2=None,
                                                      op0=ALU.mult), r=['nega'], w=['nega'])

                def conv_block(jb):
                    P = PB[jb % 2]
                    for c in range(6):
                        i2 = c % 2
                        bk = 5 + i2
                        for tau in range(5):
                            S.op('pe', lambda e, c=c, tau=tau, P=P, bk=bk: e.matmul(
                                pb[bk][:], lhsT=dg[:, c * 5 + tau, :], rhs=P[:, c, tau:tau + 512],
                                start=(tau == 0), stop=(tau == 4)),
                                r=[('dg', c), ('PB', jb % 2)], w=[('pb', bk)])
                        S.op('act', lambda e, i2=i2, bk=bk: e.activation(out=cv[i2][:], in_=pb[bk][:], func=AF.Silu),
                             r=[('pb', bk)], w=[('cv', i2)])
                        dst = qkv_s[c * 128:(c + 1) * 128, jb * 512:(jb + 1) * 512]
                        if c % 3 == 2:
                            S.op('sp', lambda e, i2=i2, dst=dst: e.dma_start(out=dst, in_=cv[i2][:]),
                                 r=[('cv', i2)], w=[('qkv', c, jb)], dma='qs%d' % i2)
                        else:
                            S.op('pool', lambda e, i2=i2: e.tensor_tensor(out=sq[i2][:], in0=cv[i2][:], in1=cv[i2][:],
                                                                         op=ALU.mult),
                                 r=[('cv', i2)], w=[('sq', i2)])
                            S.op('pe', lambda e, i2=i2: e.matmul(pb[7][:], lhsT=onesr[:], rhs=sq[i2][:],
                                                                 start=True, stop=True),
                                 r=['onesr', ('sq', i2)], w=[('pb', 7)])
                            S.op('act', lambda e, i2=i2: e.activation(out=rn[i2][:], in_=pb[7][:], func=AF.Sqrt,
                                                                      bias=EPS),
                                 r=[('pb', 7)], w=[('rn', i2)])
                            S.op('dve', lambda e, i2=i2: e.reciprocal(out=rn[i2][:], in_=rn[i2][:]),
                                 r=[('rn', i2)], w=[('rn', i2)])
                            sc = (128.0 ** -0.5) if c % 3 == 0 else 1.0
                            S.op('dve', lambda e, i2=i2, sc=sc: e.scalar_tensor_tensor(
                                out=qo[i2][:], in0=cv[i2][:], scalar=sc, in1=rn[i2][:], op0=ALU.mult, op1=ALU.mult),
                                r=[('cv', i2), ('rn', i2)], w=[('qo', i2)])
                            S.op('sp', lambda e, i2=i2, dst=dst: e.dma_start(out=dst, in_=qo[i2][:]),
                                 r=[('qo', i2)], w=[('qkv', c, jb)], dma='qo%d' % i2)

                for jb in range(16):
                    hi = jb % 2
                    r_, off = jb // 4, (jb % 4) * 512
                    for ci in range(8):
                        src = ha[ci][r_ * 256:(r_ + 1) * 256, :].rearrange("(k p) t -> p k t", p=128)[:, :, off:off + 512]
                        S.op('sp', lambda e, hi=hi, src=src, ci=ci: e.dma_start(out=hblk[hi][:, 2 * ci:2 * ci + 2, :],
                                                                               in_=src),
                             r=[('ha', ci)], w=[('hblk', hi)], dma='hblk%d' % hi)
                    P = PB[jb % 2]
                    for c in range(6):
                        bk = c % 4
                        for kc in range(KC):
                            S.op('pe', lambda e, c=c, kc=kc, bk=bk, hi=hi: e.matmul(
                                pb[bk][:], lhsT=wdnb[:, c, kc, :], rhs=hblk[hi][:, kc, :],
                                start=(kc == 0), stop=(kc == KC - 1)),
                                r=[('wdnb', c), ('hblk', hi)], w=[('pb', bk)])
                        S.op('act', lambda e, c=c, bk=bk, P=P: e.copy(out=P[:, c, 2:514], in_=pb[bk][:]),
                             r=[('pb', bk)], w=[('PB', jb % 2)])
                    for k in range(4):
                        for kc in range(KC):
                            S.op('pe', lambda e, k=k, kc=kc, hi=hi: e.matmul(
                                pb[4][:, k * 8:(k + 1) * 8], lhsT=hblk[hi][:, kc, k * 128:(k + 1) * 128],
                                rhs=wgb[:, kc, :], start=(kc == 0), stop=(kc == KC - 1)),
                                r=['wgb', ('hblk', hi)], w=[('pb', 4)])
                    gsl = gates[:, jb * 4:(jb + 1) * 4, :]
                    pg = pb[4][:, 0:32].rearrange("p (k g) -> p k g", k=4)
                    S.op('dve', lambda e, pg=pg: e.tensor_tensor(
                        out=gt[:, :, 0:4], in0=pg[:, :, 0:4], in1=dtt[:].unsqueeze(1).broadcast_to([128, 4, 4]),
                        op=ALU.add), r=[('pb', 4), 'dtt'], w=['gt'])
                    S.op('act', lambda e: e.activation(out=gt[:, :, 0:4], in_=gt[:, :, 0:4], func=AF.Exp),
                         r=['gt'], w=['gt'])
                    S.op('act', lambda e: e.activation(out=gt[:, :, 0:4], in_=gt[:, :, 0:4], func=AF.Ln, bias=1.0),
                         r=['gt'], w=['gt'])
                    S.op('dve', lambda e, gsl=gsl: e.tensor_tensor(
                        out=gsl[:, :, 0:4], in0=gt[:, :, 0:4], in1=nega[:].unsqueeze(1).broadcast_to([128, 4, 4]),
                        op=ALU.mult), r=['gt', 'nega'], w=['gates'])
                    S.op('act', lambda e, gsl=gsl, pg=pg: e.activation(out=gsl[:, :, 4:8], in_=pg[:, :, 4:8],
                                                                       func=AF.Sigmoid),
                         r=[('pb', 4)], w=['gates'])
                    if jb > 0:
                        Pp = PB[(jb - 1) % 2]
                        S.op('pool', lambda e, P=P, Pp=Pp: e.tensor_copy(out=Pp[:, :, 514:516],
                                                                         in_=(P[:, :, 2:4].bitcast(F32) if RT == F32R else P[:, :, 2:4])),
                             r=[('PB', jb % 2)], w=[('PB', (jb - 1) % 2)])
                        S.op('pool', lambda e, P=P, Pp=Pp: e.tensor_copy(out=P[:, :, 0:2],
                                                                         in_=(Pp[:, :, 512:514].bitcast(F32) if RT == F32R else Pp[:, :, 512:514])),
                             r=[('PB', (jb - 1) % 2)], w=[('PB', jb % 2)])
                        conv_block(jb - 1)
                    else:
                        S.op('dve', lambda e, P=P: e.tensor_scalar(out=P[:, :, 0:2], in0=cwt[:, :, 0:2], scalar1=0.0,
                                                                   scalar2=None, op0=ALU.mult), r=['cwt'], w=[('PB', 0)])
                Pl = PB[15 % 2]
                S.op('dve', lambda e, Pl=Pl: e.tensor_scalar(out=Pl[:, :, 514:516], in0=cwt[:, :, 0:2], scalar1=0.0,
                                                             scalar2=None, op0=ALU.mult), r=['cwt'], w=[('PB', 15 % 2)])
                conv_block(15)
            S.barrier()
            if STOP < 4:
                continue

            run_d2(gates, qkv_s, os_, NTS)
            if DBG and l == 0:
                S.barrier()
                S.op('sp', lambda e: e.dma_start(out=dbg_g, in_=gates[:]), r=['gates'], w=['dbg_g'], dma='dbg')
                S.op('sp', lambda e: e.dma_start(out=dbg_qkv, in_=qkv_s), w=['dbg_qkv'], dma='dbg')
                for ci in range(8):
                    S.op('sp', lambda e, ci=ci: e.dma_start(out=dbg_o[ci * 1024:(ci + 1) * 1024, :], in_=os_[ci]),
                         w=[('dbg_o', ci)], dma='dbg')
            S.barrier()
        if STOP < 5:
            continue
        for ci in range(8):
            S.op('pool', lambda e, ci=ci: e.collective_compute("AllGather", ALU.bypass, replica_groups=RG,
                                                               ins=[os_[ci].opt()], outs=[oa[ci].opt()]),
                 r=[('osend', n) for n in range(ci * 8, ci * 8 + 8)], w=[('oa', ci)], cc='ag2')

        if STOP < 6:
            continue
        with ExitStack() as st:
            vy = sb("vy", [128, NT, 1024], BF16, st)
            yaT = sb("yaT", [128, 8, T], BF16, st)
            wst = [sb("wst%d" % i, [128, KC, 128], F32, st) for i in range(2)]
            wbf = [sb("wbf%d" % i, [128, KC, 128], BF16, st) for i in range(4)]
            nwt = sb("nwt", [128, KC], F32, st)
            swT_s = sb("swT_s", [128, 8, 128], F32, st)
            swT = sb("swT", [128, 8, 128], BF16, st)
            gu = [sb("gu%d" % i, [128, 512], F32, st) for i in range(2)]
            sz = [sb("sz%d" % i, [128, 512], F32, st) for i in range(2)]
            t2 = [sb("t2_%d" % i, [128, 512], F32, st) for i in range(2)]
            st6 = sb("st6", [128, NT, 8, 6], F32, st)
            mv = sb("mv", [128, NT, 2], F32, st)
            lrs = sb("lrs", [128, NT], F32, st)
            ltmp = sb("ltmp", [128, NT], F32, st)
            sgubt = sb("sgubt", [128, 8, 128], F32, st)
            S.op('sp', lambda e: e.dma_start(out=nwt[:], in_=normw[l]), w=['nwt'], dma='c0')
            S.op('sp', lambda e: e.dma_start(out=swT_s[:], in_=sguwT[l]), w=['swT_s'], dma='c1')
            S.op('sp', lambda e: e.dma_start(out=sgubt[:], in_=sgub[l]), w=['sgubt'], dma='c2')
            S.op('dve', lambda e: e.tensor_copy(out=swT[:], in_=swT_s[:]), r=['swT_s'], w=['swT'])

            wq = []

            def wload(bi):
                i = bi % 2
                j = bi % 4
                S.op('sp', lambda e, bi=bi, i=i: e.dma_start(out=wst[i][:], in_=wsg[l, bi]), w=[('wst', i)],
                     dma='wst%d' % i)
                S.op('pool', lambda e, i=i, j=j: e.tensor_tensor(
                    out=wbf[j][:], in0=wst[i][:], in1=nwt[:].unsqueeze(2).broadcast_to([128, KC, 128]),
                    op=ALU.mult), r=[('wst', i), 'nwt'], w=[('wbf', j)])

            def proj(bi, tb, bk):
                j = bi % 4
                for kc in range(KC):
                    S.op('pe', lambda e, kc=kc, j=j, tb=tb, bk=bk: e.matmul(
                        pb[bk][:], lhsT=wbf[j][:, kc, :], rhs=xT[:, kc, tb * 512:(tb + 1) * 512],
                        start=(kc == 0), stop=(kc == KC - 1)),
                        r=[('wbf', j)] + [('xT', tt) for tt in range(tb * 4, tb * 4 + 4)], w=[('pb', bk)])

            wload(0)
            wload(1)
            with ExitStack() as stv:
                lngt = sb("lngt", [128, 1024], F32, stv)
                lnbt = sb("lnbt", [128, 1024], F32, stv)
                S.op('sp', lambda e: e.dma_start(out=lngt[:], in_=lng[l]), w=['lngt'], dma='c3')
                S.op('sp', lambda e: e.dma_start(out=lnbt[:], in_=lnb[l]), w=['lnbt'], dma='c4')
                cnt = 0
                for vbk in range(8):
                    if vbk + 2 < 32:
                        wload(vbk + 2)
                    for tb in range(4):
                        bk = cnt % 4
                        gi = cnt % 2
                        cnt += 1
                        proj(vbk, tb, bk)
                        S.op('act', lambda e, bk=bk, gi=gi: e.activation(out=gu[gi][:], in_=pb[bk][:], func=AF.Gelu),
                             r=[('pb', bk)], w=[('gu', gi)])
                        tbk = 4 + gi
                        trv = pb[tbk][:].rearrange("p (a c) -> p a c", a=4)
                        for k in range(4):
                            S.op('pe', lambda e, k=k, gi=gi, trv=trv: e.transpose(
                                out=trv[:, k, :], in_=gu[gi][:, k * 128:(k + 1) * 128], identity=IDENT),
                                r=[('gu', gi), 'cst'], w=[('pb', tbk)])
                        for k in range(4):
                            t = tb * 4 + k
                            S.op('dve', lambda e, k=k, t=t, vbk=vbk, trv=trv: e.bn_stats(out=st6[:, t, vbk, :],
                                                                                         in_=trv[:, k, :]),
                                 r=[('pb', tbk)], w=[('st6', t)])
                        S.op('act', lambda e, tb=tb, vbk=vbk, trv=trv: e.copy(
                            out=vy[:, tb * 4:(tb + 1) * 4, vbk * 128:(vbk + 1) * 128], in_=trv),
                            r=[('pb', tbk)], w=[('vy', tt) for tt in range(tb * 4, tb * 4 + 4)])
                for t in range(NT):
                    S.op('dve', lambda e, t=t: e.bn_aggr(out=mv[:, t, :],
                                                         in_=st6[:, t].rearrange("p a b -> p (a b)")),
                         r=[('st6', t)], w=[('mv', t)])
                    rsqrt_chain(mv[:, t, 1:2], lrs[:, t:t + 1], 1.0, [('mv', t)], ('lrs', t), ltmp[:, t:t + 1],
                                ('ltmp', t))
                    S.op('dve', lambda e, t=t: e.tensor_scalar(out=vy[:, t, :], in0=vy[:, t, :], scalar1=mv[:, t, 0:1],
                                                               scalar2=lrs[:, t:t + 1], op0=ALU.subtract,
                                                               op1=ALU.mult),
                         r=[('vy', t), ('mv', t), ('lrs', t)], w=[('vy', t)])
                    S.op('pool', lambda e, t=t: e.tensor_tensor(out=vy[:, t, :], in0=vy[:, t, :], in1=lngt[:],
                                                                op=ALU.mult), r=[('vy', t), 'lngt'], w=[('vy', t)])
                    S.op('pool', lambda e, t=t: e.tensor_tensor(out=vy[:, t, :], in0=vy[:, t, :], in1=lnbt[:],
                                                                op=ALU.add), r=[('vy', t), 'lnbt'], w=[('vy', t)])
                if DBG and l == 0:
                    S.op('sp', lambda e: e.dma_start(out=dbg_vn, in_=vy[:]), r=[('vy', t) for t in range(NT)],
                         w=['dbg_vn'], dma='dbg')
                cnt = 0
                for g in range(8):
                    bu, bz = 8 + 2 * g, 9 + 2 * g
                    for nb in (bu + 2, bz + 2):
                        if nb < 32:
                            wload(nb)
                    for tb in range(4):
                        gi = cnt % 2
                        cnt += 1
                        proj(bu, tb, 0 + gi)
                        proj(bz, tb, 2 + gi)
                        S.op('act', lambda e, gi=gi: e.activation(out=gu[gi][:], in_=pb[0 + gi][:], func=AF.Gelu),
                             r=[('pb', 0 + gi)], w=[('gu', gi)])
                        S.op('act', lambda e, gi=gi: e.activation(out=sz[gi][:], in_=pb[2 + gi][:], func=AF.Silu),
                             r=[('pb', 2 + gi)], w=[('sz', gi)])
                        spb = 6 + gi
                        for k in range(4):
                            t = tb * 4 + k
                            S.op('pe', lambda e, k=k, t=t, g=g, spb=spb: e.matmul(
                                pb[spb][:, k * 128:(k + 1) * 128], lhsT=vy[:, t, g * 128:(g + 1) * 128],
                                rhs=swT[:, g, :], start=True, stop=True),
                                r=[('vy', t), 'swT'], w=[('pb', spb)])
                        S.op('pool', lambda e, gi=gi: e.tensor_tensor(out=gu[gi][:], in0=gu[gi][:], in1=sz[gi][:],
                                                                      op=ALU.mult),
                             r=[('gu', gi), ('sz', gi)], w=[('gu', gi)])
                        S.op('dve', lambda e, gi=gi, g=g, spb=spb: e.tensor_tensor(
                            out=t2[gi][:].rearrange("p (a c) -> p a c", a=4),
                            in0=pb[spb][:].rearrange("p (a c) -> p a c", a=4),
                            in1=sgubt[:, g, :].unsqueeze(1).broadcast_to([128, 4, 128]), op=ALU.add),
                             r=[('pb', spb), 'sgubt'], w=[('t2', gi)])
                        S.op('dve', lambda e, gi=gi, g=g, tb=tb: e.tensor_tensor(
                            out=yaT[:, g, tb * 512:(tb + 1) * 512], in0=t2[gi][:], in1=gu[gi][:], op=ALU.mult),
                            r=[('t2', gi), ('gu', gi)], w=[('yaT', g)])
            S.barrier()
            onT = vy[:].rearrange("p a b -> p (a b)").rearrange("p (j t) -> p j t", j=8)
            with ExitStack() as sto:
                og = [sb("og%d" % i, [128, 1024], F32, sto) for i in range(2)]
                oc = sb("oc", [128, 1024], F32, sto)
                ojk = sb("ojk", [128, 128], BF16, sto)
                oss = sb("oss", [128, 8], F32, sto)
                otmp = sb("otmp", [128, 8], F32, sto)
                ors = sb("ors", [128, 8], F32, sto)
                ocnt = 0
                for t in range(NT):
                    for q in range(4):
                        oi = ocnt % 2
                        ocnt += 1
                        oci = 2 * q + t // 8
                        src = oa[oci].rearrange("(r k) c -> k r c", r=4)[(t % 8) * 128:(t % 8 + 1) * 128]
                        S.op('sp', lambda e, oi=oi, src=src: e.dma_start(
                            out=og[oi][:].rearrange("p (r c) -> p r c", r=4), in_=src),
                            r=[('oa', oci)], w=[('og', oi)], dma='og%d' % oi)
                        if q == 0:
                            S.op('dve', lambda e, oi=oi: e.tensor_scalar(out=oc[:], in0=og[oi][:], scalar1=selt[:, 0:1],
                                                                         scalar2=None, op0=ALU.mult),
                                 r=[('og', oi), 'selt'], w=['oc'])
                        else:
                            S.op('dve', lambda e, q=q, oi=oi: e.scalar_tensor_tensor(
                                out=oc[:], in0=og[oi][:], scalar=selt[:, q:q + 1], in1=oc[:], op0=ALU.mult,
                                op1=ALU.add), r=[('og', oi), 'selt', 'oc'], w=['oc'])
                    for h in range(8):
                        S.op('act', lambda e, h=h: e.activation(out=ojk[:], in_=oc[:, h * 128:(h + 1) * 128],
                                                                func=AF.Square, accum_out=oss[:, h:h + 1]),
                             r=['oc'], w=['ojk', 'oss'])
                    rsqrt_chain(oss[:], ors[:], 1.0 / 128, ['oss'], 'ors', otmp[:], 'otmp')
                    S.op('dve', lambda e: e.tensor_tensor(
                        out=oc[:].rearrange("p (h c) -> p h c", h=8), in0=oc[:].rearrange("p (h c) -> p h c", h=8),
                        in1=ors[:].unsqueeze(2).broadcast_to([128, 8, 128]), op=ALU.mult),
                        r=['oc', 'ors'], w=['oc'])
                    for hb in range(2):
                        bk = 2 + hb
                        trv = pb[bk][:].rearrange("p (a c) -> p a c", a=4)
                        for k in range(4):
                            j = hb * 4 + k
                            S.op('pe', lambda e, k=k, j=j, trv=trv: e.transpose(
                                out=trv[:, k, :], in_=oc[:, j * 128:(j + 1) * 128], identity=IDENT),
                                r=['oc', 'cst'], w=[('pb', bk)])
                        S.op('act', lambda e, hb=hb, t=t, trv=trv: e.copy(
                            out=onT[:, hb * 4:(hb + 1) * 4, t * 128:(t + 1) * 128], in_=trv),
                            r=[('pb', bk)], w=[('onT', t)])
            cnt = 0
            for j in range(8):
                bi = 24 + j
                if bi + 2 < 32:
                    wload(bi + 2)
                for tb in range(4):
                    gi = cnt % 2
                    cnt += 1
                    proj(bi, tb, 0 + gi)
                    S.op('act', lambda e, gi=gi: e.activation(out=sz[gi][:], in_=pb[0 + gi][:], func=AF.Silu),
                         r=[('pb', 0 + gi)], w=[('sz', gi)])
                    S.op('dve', lambda e, gi=gi, j=j, tb=tb: e.scalar_tensor_tensor(
                        out=onT[:, j, tb * 512:(tb + 1) * 512], in0=onT[:, j, tb * 512:(tb + 1) * 512],
                        scalar=dnwt[:, l:l + 1], in1=sz[gi][:], op0=ALU.mult, op1=ALU.mult),
                        r=[('onT', tt) for tt in range(tb * 4, tb * 4 + 4)] + [('sz', gi), 'dnwt'],
                        w=[('ybT', j, tb)])
            S.barrier()
            if DBG and l == 0:
                S.op('sp', lambda e: e.dma_start(out=dbg_ya, in_=yaT[:]), w=['dbg_ya'], dma='dbg')
                S.op('sp', lambda e: e.dma_start(out=dbg_yb, in_=onT), w=['dbg_yb'], dma='dbg')
                S.barrier()
            xTf = xT[:].rearrange("p a b -> p (a b)").bitcast(F32)
            wos = xTf[:, 0:8192].rearrange("p (k c) -> p k c", k=KC)
            wob = xT[:].rearrange("p a b -> p (a b)")[:, 16384:16384 + 8192].rearrange("p (k c) -> p k c", k=KC)
            with ExitStack() as sto:
                xo = [sb("xo%d" % i, [128, 512], F32, sto) for i in range(2)]
                xn = [sb("xn%d" % i, [128, 512], F32, sto) for i in range(2)]
                cnt = 0
                for jb in range(4):
                    S.op('sp', lambda e, jb=jb: e.dma_start(out=wos, in_=wout[l, jb]), w=['wos'] + ALLXT, dma='wos')
                    S.op('pool', lambda e: e.tensor_copy(out=wob, in_=wos), r=['wos'], w=['wob'] + ALLXT)
                    for t in range(NT):
                        i = cnt % 2
                        bk = 4 + cnt % 4
                        cnt += 1
                        S.op('sp', lambda e, t=t, jb=jb, i=i: e.dma_start(
                            out=xo[i][:], in_=x_src[t * 128:(t + 1) * 128, jb * 512:(jb + 1) * 512]),
                            r=[('xres', t)], w=[('xo', i)], dma='xo%d' % i)
                        for c in range(16):
                            lhs = yaT[:, c, t * 128:(t + 1) * 128] if c < 8 else onT[:, c - 8, t * 128:(t + 1) * 128]
                            rk = [('yaT', c)] if c < 8 else [('ybT', c - 8, t // 4)]
                            S.op('pe', lambda e, c=c, lhs=lhs, bk=bk: e.matmul(pb[bk][:], lhsT=lhs, rhs=wob[:, c, :],
                                                                              start=(c == 0), stop=(c == 15)),
                                 r=rk + ['wob'], w=[('pb', bk)])
                        S.op('dve', lambda e, i=i, bk=bk: e.tensor_tensor(out=xn[i][:], in0=pb[bk][:], in1=xo[i][:],
                                                                          op=ALU.add),
                             r=[('pb', bk), ('xo', i)], w=[('xn', i)])
                        S.op('sp', lambda e, t=t, jb=jb, i=i: e.dma_start(
                            out=x_res[t * 128:(t + 1) * 128, jb * 512:(jb + 1) * 512], in_=xn[i][:]),
                            r=[('xn', i)], w=[('xres2', t, jb)], dma='xn%d' % i)
            for t in range(NT):
                S.state[('xres', t)] = [None, {}]
            S.barrier()
        S.new_epoch()

    with ExitStack() as st:
        xin = [sb("fxin%d" % i, [128, D], F32, st) for i in range(2)]
        xot = [sb("fxo%d" % i, [128, D], F32, st) for i in range(2)]
        junk = sb("fjunk", [128, D], BF16, st)
        fnt = sb("fnt", [128, D], F32, st)
        ssq = sb("fssq", [128, NT], F32, st)
        rtmp = sb("frtmp", [128, NT], F32, st)
        rstd = sb("frstd", [128, NT], F32, st)
        S.op('sp', lambda e: e.dma_start(out=fnt[:], in_=fnw), w=['fnt'], dma='c0')
        x_src = x_res if depth > 0 else x_in
        otoks = []
        for t in range(NT):
            i = t % 2
            S.op('sp', lambda e, t=t, i=i: e.dma_start(out=xin[i][:], in_=x_src[t * 128:(t + 1) * 128, :]),
                 w=[('fxin', i)], dma='fxin%d' % i)
            S.op('act', lambda e, t=t, i=i: e.activation(out=junk[:], in_=xin[i][:], func=AF.Square,
                                                         accum_out=ssq[:, t:t + 1]),
                 r=[('fxin', i)], w=['fjunk', ('fssq', t)])
            rsqrt_chain(ssq[:, t:t + 1], rstd[:, t:t + 1], 1.0 / D, [('fssq', t)], ('frstd', t), rtmp[:, t:t + 1],
                        ('frtmp', t))
            S.op('dve', lambda e, t=t, i=i: e.scalar_tensor_tensor(out=xot[i][:], in0=xin[i][:],
                                                                   scalar=rstd[:, t:t + 1], in1=fnt[:],
                                                                   op0=ALU.mult, op1=ALU.mult),
                 r=[('fxin', i), ('frstd', t), 'fnt'], w=[('fxo', i)])
            otoks.append(S.op('sp', lambda e, t=t, i=i: e.dma_start(out=out[t * 128:(t + 1) * 128, :], in_=xot[i][:]),
                              r=[('fxo', i)], w=[('out', t)], dma='fxo%d' % i))
    S.barrier()
    S.emit()
    stack.close()
    return nc


def _consts():
    c = np.zeros((128, 8, 128), np.float32)
    p = np.arange(128)[:, None]
    f = np.arange(128)[None, :]
    c[:, C_ID] = (p == f)
    c[:, C_ONE] = 1.0
    c[:, C_TRF] = (p <= f)
    c[:, C_TRB] = (p >= f)
    c[:, C_N1F] = np.where(f >= p, 0.0, -BIG)
    c[:, C_N1B] = np.where(f <= p, 0.0, -BIG)
    c[:, C_P2F] = np.where(f < p, 0.0, BIG)
    c[:, C_P2B] = np.where(f > p, 0.0, BIG)
    return c


def _blk(w, c0, n=128):
    return np.ascontiguousarray(w[:, c0:c0 + n].reshape(KC, 128, n).transpose(1, 0, 2))


def prep_inputs(x, norm_w, w_in, sgu_ln_g, sgu_ln_b, sgu_w, sgu_b, conv_w, a_log_f, a_log_b, dt_bias_f,
                dt_bias_b, dn_norm_w, w_out, final_norm_w):
    f = np.float32
    x = np.asarray(x, f)
    w_in = np.asarray(w_in, f)
    w_out = np.asarray(w_out, f)
    rep = lambda v, n=128: np.ascontiguousarray(np.broadcast_to(np.asarray(v, f)[None], (n,) + np.asarray(v).shape))
    shared = {}
    shared["normw"] = np.ascontiguousarray(np.stack([np.asarray(norm_w[l], f).reshape(KC, 128).T for l in range(L)]))
    cols = [1024 + 128 * g for g in range(8)]
    for g in range(8):
        cols += [128 * g, 2048 + 128 * g]
    cols += [6144 + 128 * j for j in range(8)]
    shared["wsg"] = np.ascontiguousarray(np.stack([np.stack([_blk(w_in[l], c) for c in cols]) for l in range(L)]))
    shared["wout"] = np.ascontiguousarray(
        np.stack([np.stack([_blk(w_out[l], j * 512, 512) for j in range(4)]) for l in range(L)]))
    shared["lng"] = np.stack([rep(sgu_ln_g[l]) for l in range(L)])
    shared["lnb"] = np.stack([rep(sgu_ln_b[l]) for l in range(L)])
    shared["sguwT"] = np.ascontiguousarray(np.stack([np.asarray(sgu_w[l], f).transpose(2, 0, 1) for l in range(L)]))
    shared["sgub"] = np.ascontiguousarray(np.stack([rep(np.asarray(sgu_b[l], f)) for l in range(L)]))
    shared["dnw"] = np.ascontiguousarray(np.asarray(dn_norm_w, f).T)
    shared["fnw"] = rep(final_norm_w)
    shared["consts"] = _consts()
    in_maps = []
    for c in range(8):
        b, s = c // 4, c % 4
        hs = (2 * s, 2 * s + 1)
        m = dict(shared)
        m["x"] = np.ascontiguousarray(x[b, s * T:(s + 1) * T, :])
        m["wdn"] = np.ascontiguousarray(np.stack([np.stack(
            [_blk(w_in[l], 3072 + ty * 1024 + h * 128) for h in hs for ty in range(3)]) for l in range(L)]))
        gcols = [7168 + hs[0], 7176 + hs[0], 7168 + hs[1], 7176 + hs[1],
                 7184 + hs[0], 7192 + hs[0], 7184 + hs[1], 7192 + hs[1]]
        m["wdg"] = np.ascontiguousarray(np.stack(
            [w_in[l][:, gcols].reshape(KC, 128, 8).transpose(1, 0, 2) for l in range(L)]))
        cw = np.asarray(conv_w, f)
        m["convw"] = np.ascontiguousarray(np.stack([np.stack(
            [cw[l][:, ty * 1024 + h * 128: ty * 1024 + (h + 1) * 128].T for h in hs for ty in range(3)], axis=1)
            for l in range(L)]))
        al = np.stack([np.array([a_log_f[l][hs[0]], a_log_b[l][hs[0]], a_log_f[l][hs[1]], a_log_b[l][hs[1]]], f)
                       for l in range(L)])
        db = np.stack([np.array([dt_bias_f[l][hs[0]], dt_bias_b[l][hs[0]], dt_bias_f[l][hs[1]], dt_bias_b[l][hs[1]]], f)
                       for l in range(L)])
        m["alog"] = np.ascontiguousarray(np.broadcast_to(al[:, None, :], (L, 128, 4)))
        m["dtb"] = np.ascontiguousarray(np.broadcast_to(db[:, None, :], (L, 128, 4)))
        sl = np.zeros((128, 4), f)
        sl[:, s] = 1.0
        m["sel"] = sl
        in_maps.append(m)
    return in_maps


_LAST = {}


def kernel(**inputs):
    inputs = {k: np.asarray(v) for k, v in inputs.items()}
    in_maps = prep_inputs(**inputs)
    depth = int(os.environ.get("MK_DEPTH", L))
    nc = build(depth)
    res = run_bass_kernel_spmd(nc, in_maps, core_ids=list(range(8)))
    if DBG:
        _LAST["res"] = res.results[0]
    outp = np.zeros((2, SEQ, D), np.float32)
    for c in range(8):
        b, s = c // 4, c % 4
        outp[b, s * T:(s + 1) * T, :] = res.results[c]["out"]
    return outp
```

```python
import os
import numpy as np
from contextlib import ExitStack
import concourse.bass as bass
import concourse.mybir as mybir
from concourse.bass_utils import run_bass_kernel_spmd

F32 = mybir.dt.float32
F32R = mybir.dt.float32r
BF16 = mybir.dt.bfloat16
AF = mybir.ActivationFunctionType
ALU = mybir.AluOpType
AX = mybir.AxisListType

D = 2048
L = 4
T = 2048
NT = 16
SEQ = 8192
KC = 16
EPS = 1e-6
NTS = SEQ // 128
RT = BF16
RT2 = BF16 if os.environ.get('MK_RT2', 'f32r') == 'bf16' else F32R
BIG = 30000.0
DBG = int(os.environ.get("MK_DBG", 0))
STOP = int(os.environ.get("MK_STOP", 9))

C_ID, C_ONE, C_TRF, C_TRB, C_N1F, C_N1B, C_P2F, C_P2B = range(8)


class _Rec:
    def __getattr__(self, name):
        def f(*a, **k):
            self.call = (name, a, k)
            return self
        return f


class Sched:
    def __init__(self, nc, stack):
        self.nc = nc
        self.stack = stack
        self.names = ['pe', 'dve', 'act', 'pool', 'sp']
        self.ops = {k: [] for k in self.names}
        self.epoch = 0
        self.psem = {k: stack.enter_context(nc.semaphore("p_" + k + "_0")) for k in self.names}
        self.pkey = {k: "p_%s_0" % k for k in self.names}
        self.cnt = {k: 0 for k in self.names}
        self.waited = {k: {} for k in self.names}
        self.state = {}
        self.dsem = {}

    def new_epoch(self):
        self.epoch += 1
        for k in self.names:
            self.psem[k] = self.stack.enter_context(self.nc.semaphore("p_%s_%d" % (k, self.epoch)))
            self.pkey[k] = "p_%s_%d" % (k, self.epoch)
            self.cnt[k] = 0

    def _slot(self, slot):
        if slot not in self.dsem:
            self.dsem[slot] = [self.stack.enter_context(self.nc.semaphore("d_" + str(slot))), 0]
        return self.dsem[slot]

    def op(self, eng, fn, r=(), w=(), dma=None, cc=None):
        r2, w2 = [], []
        for b in r:
            if isinstance(b, tuple) and b[0] == 'pb':
                if ('pb', b[1]) not in w2:
                    w2.append(('pb', b[1]))
            else:
                r2.append(b)
        for b in w:
            if isinstance(b, tuple) and b[0] == 'pb':
                b = ('pb', b[1])
            if b not in w2:
                w2.append(b)
        r, w = r2, w2
        need = {}

        def add(tok):
            if tok is not None and need.get(tok[0], (None, -1))[1] < tok[1]:
                need[tok[0]] = tok
        for b in r:
            st = self.state.get(b)
            if st:
                add(st[0])
        for b in w:
            st = self.state.get(b)
            if st:
                add(st[0])
                for t in st[1].values():
                    add(t)
        waits = []
        for k, tok in need.items():
            if eng == 'pe' and k.startswith('p_pe_'):
                continue
            if self.waited[eng].get(k, -1) >= tok[1]:
                continue
            self.waited[eng][k] = tok[1]
            waits.append((tok[2], tok[1]))
        if dma is not None:
            sl = self._slot(dma)
            sl[1] += 16
            tok = ("d_" + str(dma), sl[1], sl[0])
            inc = (sl[0], 16)
        elif cc is not None:
            sl = self._slot(cc)
            sl[1] += 1
            tok = ("d_" + str(cc), sl[1], sl[0])
            inc = (sl[0], None)
        else:
            self.cnt[eng] += 1
            tok = (self.pkey[eng], self.cnt[eng], self.psem[eng])
            inc = (self.psem[eng], 1)
        rec = _Rec()
        fn(rec)
        self.ops[eng].append((waits, rec.call, inc))
        for b in w:
            self.state[b] = [tok, {}]
        for b in r:
            if b in w:
                continue
            st = self.state.setdefault(b, [None, {}])
            st[1][tok[0]] = tok
        return tok

    def barrier(self):
        toks = []
        for k in self.names:
            if self.cnt[k] > 0:
                toks.append((self.pkey[k], self.cnt[k], self.psem[k]))
        for slot, (sem, cum) in self.dsem.items():
            if cum > 0 and slot not in ('ag1', 'ag2'):
                toks.append(("d_" + str(slot), cum, sem))
        for eng in self.names:
            waits = []
            for tok in toks:
                if tok[0] == self.pkey[eng]:
                    continue
                if self.waited[eng].get(tok[0], -1) >= tok[1]:
                    continue
                self.waited[eng][tok[0]] = tok[1]
                waits.append((tok[2], tok[1]))
            if waits:
                self.ops[eng].append((waits, None, None))

    def emit(self):
        nc = self.nc
        ops = self.ops

        def run(k, e):
            for waits, fn, inc in ops[k]:
                for sem, val in waits:
                    e.wait_ge(sem, val)
                if fn is None:
                    continue
                ins = getattr(e, fn[0])(*fn[1], **fn[2])
                if inc[1] is None:
                    ins.then_inc(inc[0])
                else:
                    ins.then_inc(inc[0], inc[1])
        with nc.Block() as block:
            @block.sync
            def _(e):
                run('sp', e)

            @block.tensor
            def _(e):
                run('pe', e)

            @block.vector
            def _(e):
                run('dve', e)

            @block.scalar
            def _(e):
                run('act', e)

            @block.gpsimd
            def _(e):
                run('pool', e)


def build(depth=L, test=None):
    nc = bass.Bass("TRN2", target_bir_lowering=False)
    stack = ExitStack()
    S = Sched(nc, stack)

    def din(name, shape, dt=F32):
        return nc.dram_tensor(name, list(shape), dt, kind="ExternalInput").ap()

    if test is None:
        x_in = din("x", [T, D])
        normw = din("normw", [L, 128, KC])
        wsg = din("wsg", [L, 32, 128, KC, 128])
        wdn = din("wdn", [L, 6, 128, KC, 128])
        wdg = din("wdg", [L, 128, KC, 8])
        wout = din("wout", [L, 4, 128, KC, 512])
        lng = din("lng", [L, 128, 1024])
        lnb = din("lnb", [L, 128, 1024])
        sguwT = din("sguwT", [L, 128, 8, 128])
        sgub = din("sgub", [L, 128, 8, 128])
        convw = din("convw", [L, 128, 6, 5])
        alog = din("alog", [L, 128, 4])
        dtb = din("dtb", [L, 128, 4])
        dnw = din("dnw", [128, L])
        fnw = din("fnw", [128, D])
        sel = din("sel", [128, 4])
        consts = din("consts", [128, 8, 128])
        out = nc.dram_tensor("out", [T, D], F32, kind="ExternalOutput").ap()

        x_res = nc.dram_tensor("x_res", [T, D], F32).ap()
        hs = [nc.dram_tensor("hs%d" % i, [64 * KC, 512], BF16).ap() for i in range(8)]
        ha = [nc.dram_tensor("ha%d" % i, [4 * 64 * KC, 512], BF16).ap() for i in range(8)]
        qkv_s = nc.dram_tensor("qkv_s", [6 * 128, SEQ], F32).ap()
        os_ = [nc.dram_tensor("os%d" % i, [1024, 256], F32).ap() for i in range(8)]
        oa = [nc.dram_tensor("oa%d" % i, [4 * 1024, 256], F32).ap() for i in range(8)]

    else:
        consts = din("consts", [128, 8, 128])
        qkv_in = din("qkv_in", [6 * 128, test * 128])
        gates_in = din("gates_in", [128, test, 8])
        o_out = nc.dram_tensor("o_out", [test * 128, 256], F32, kind="ExternalOutput").ap()
    RG = [[0, 1, 2, 3], [4, 5, 6, 7]]
    if DBG and test is None:
        dbg_qkv = nc.dram_tensor("dbg_qkv", [6 * 128, SEQ], F32, kind="ExternalOutput").ap()
        dbg_g = nc.dram_tensor("dbg_g", [128, NTS, 8], F32, kind="ExternalOutput").ap()
        dbg_o = nc.dram_tensor("dbg_o", [SEQ, 256], F32, kind="ExternalOutput").ap()
        dbg_ya = nc.dram_tensor("dbg_ya", [128, 8, T], BF16, kind="ExternalOutput").ap()
        dbg_yb = nc.dram_tensor("dbg_yb", [128, 8, T], BF16, kind="ExternalOutput").ap()
        dbg_vn = nc.dram_tensor("dbg_vn", [128, NT, 1024], BF16, kind="ExternalOutput").ap()

    uid = [0]

    def sb(name, shape, dt=F32, st=None):
        uid[0] += 1
        return (st or stack).enter_context(nc.sbuf_tensor("%s_%d" % (name, uid[0]), list(shape), dt))

    pb = [stack.enter_context(nc.psum_tensor("pb%d" % i, [128, 512], F32)) for i in range(8)]

    cst = sb("cst", [128, 8, 128])
    identb = sb("identb", [128, 128], BF16)
    S.op('sp', lambda e: e.dma_start(out=cst[:], in_=consts), w=['cst'], dma='c0')
    if test is None:
        selt = sb("selt", [128, 4])
        dnwt = sb("dnwt", [128, L])
        xT = sb("xT", [128, KC, T], BF16)
        ALLXT = [('xT', t) for t in range(NT)]
        S.op('sp', lambda e: e.dma_start(out=selt[:], in_=sel), w=['selt'], dma='c1')
        S.op('sp', lambda e: e.dma_start(out=dnwt[:], in_=dnw), w=['dnwt'], dma='c2')
    S.op('dve', lambda e: e.tensor_copy(out=identb[:], in_=cst[:, C_ID, :]), r=['cst'], w=['identb'])
    IDENT = cst[:, C_ID, :]
    ONES = cst[:, C_ONE, :]

    def rsqrt_chain(src_ap, dst_ap, scale, keys_r, key_w, tmp_ap, tmpkey):
        S.op('dve', lambda e: e.tensor_scalar(out=tmp_ap, in0=src_ap, scalar1=scale, scalar2=EPS,
                                              op0=ALU.mult, op1=ALU.add), r=keys_r, w=[tmpkey])
        S.op('act', lambda e: e.activation(out=tmp_ap, in_=tmp_ap, func=AF.Sqrt), r=[tmpkey], w=[tmpkey])
        S.op('dve', lambda e: e.reciprocal(out=dst_ap, in_=tmp_ap), r=[tmpkey], w=[key_w])

    def run_d2(gates, qkv_s, os_, nts):
        npairs = nts // 2
        with ExitStack() as st:
            def bc(ap2, n=128):
                return ap2.unsqueeze(2).broadcast_to([128, ap2.shape[1], n])
            IDENT4 = IDENT.unsqueeze(1).broadcast_to([128, 4, 128])

            def rd(ap):
                return ap.bitcast(F32) if RT2 == F32R else ap
            qview = qkv_s.rearrange("(h c p) t -> p h c t", h=2, c=3)

            def dir_gen(d):
                pre = "c%d_" % d
                BA, BB, BC, BD = 4 * d, 4 * d + 1, 4 * d + 2, 4 * d + 3
                TRI = cst[:, C_TRF + d, :]
                NEG1_4 = cst[:, C_N1F + d, :].unsqueeze(1).broadcast_to([128, 4, 128])
                POS2_4 = cst[:, C_P2F + d, :].unsqueeze(1).broadcast_to([128, 4, 128])
                K = lambda n: pre + n

                def t4(name, dt=F32):
                    return sb(pre + name, [128, 4, 128], dt, st)

                def t2(name, dt=F32):
                    return sb(pre + name, [128, 2, 128], dt, st)
                qT = [t4("qT0"), t4("qT1")]
                kT = [t4("kT0"), t4("kT1")]
                vT = [t4("vT0"), t4("vT1")]
                kTr, ktok, vb = t4("kTr", RT2), t4("ktok"), t4("vb", RT2)
                A1, A2, scr = t4("A1"), t4("A2"), t4("scr")
                Pm = [t4("P0", RT2), t4("P1", RT2)]
                PTm = [t4("PT0", RT2), t4("PT1", RT2)]
                Y, kbg = t4("Y", RT2), t4("kbg", RT2)
                gs = sb(pre + "gs", [128, 2, 36], F32, st)
                HqT = sb(pre + "HqT", [128, 4, 2, 128], RT2, st)
                HwT = sb(pre + "HwT", [128, 4, 2, 128], RT2, st)
                Hu = sb(pre + "Hu", [128, 4, 2, 128], F32, st)
                Hkd = sb(pre + "Hkd", [128, 4, 2, 128], RT2, st)
                Hat = sb(pre + "Hat", [128, 4, 2, 128], RT2, st)
                Hgs = sb(pre + "Hgs", [128, 4, 2, 2], F32, st)
                vnew, oqs = t2("vnew", RT2), t2("oqs")
                ot = [t2("ot0"), t2("ot1")]
                oprev, St = t2("oprev"), t2("S", RT2)
                Sm = t2("Sm") if RT2 == BF16 else None
                for h in range(2):
                    S.op('dve', lambda e, h=h: e.tensor_scalar(out=St[:, h, :], in0=IDENT, scalar1=0.0, scalar2=None,
                                                               op0=ALU.mult), r=['cst'], w=[K('S')])
                    if Sm is not None:
                        S.op('dve', lambda e, h=h: e.tensor_scalar(out=Sm[:, h, :], in0=IDENT, scalar1=0.0,
                                                                   scalar2=None, op0=ALU.mult), r=['cst'], w=[K('Sm')])

                def tile_of(step):
                    return step if d == 0 else nts - 1 - step

                def h4(H, s0):
                    return H[:, s0:s0 + 2].rearrange("p a h c -> p (a h) c")

                def hk(nm, s0):
                    return [K('%s%d' % (nm, s0)), K('%s%d' % (nm, s0 + 1))]

                def load(p):
                    pp = p % 2
                    for tt in range(2):
                        n = tile_of(2 * p + tt)
                        for (buf, ty, nm) in ((qT, 0, 'qT'), (kT, 1, 'kT'), (vT, 2, 'vT')):
                            src = qview[:, :, ty, n * 128:(n + 1) * 128]
                            S.op('sp', lambda e, buf=buf, src=src, tt=tt: e.dma_start(
                                out=buf[pp][:, tt * 2:tt * 2 + 2, :], in_=src),
                                r=[('qkv', hh * 3 + ty, n // 4) for hh in range(2)], w=[K(nm + str(pp))],
                                dma=pre + nm + str(pp))

                def prep(p):
                    pp = p % 2
                    s0 = (2 * p) % 4
                    g = gs[:, pp, :]
                    GK = K('gs%d' % pp)
                    qTi, kTi, vTi = qT[pp], kT[pp], vT[pp]
                    S.op('pool', lambda e: e.tensor_copy(out=h4(HqT, s0), in_=qTi[:]), r=[K('qT%d' % pp)],
                         w=hk('HqT', s0))
                    S.op('pool', lambda e: e.tensor_copy(out=kTr[:], in_=kTi[:]), r=[K('kT%d' % pp)], w=[K('kTr')])
                    yield
                    trk = pb[BA][:].rearrange("p (a c) -> p a c", a=4)
                    trv = pb[BB][:].rearrange("p (a c) -> p a c", a=4)
                    for j in range(4):
                        S.op('pe', lambda e, j=j: e.transpose(out=trk[:, j, :], in_=kTi[:, j, :], identity=IDENT),
                             r=[K('kT%d' % pp), 'cst'], w=[('pb', BA)])
                    for j in range(4):
                        S.op('pe', lambda e, j=j: e.transpose(out=trv[:, j, :], in_=vTi[:, j, :], identity=IDENT),
                             r=[K('vT%d' % pp), 'cst'], w=[('pb', BB)])
                    yield
                    for tt in range(2):
                        n = tile_of(2 * p + tt)
                        gv = gates[:, n, :].rearrange("p (x h e) -> p x h e", x=2, h=2)
                        S.op('dve', lambda e, gv=gv, tt=tt: e.tensor_copy(out=g[:, tt * 2:tt * 2 + 2], in_=gv[:, 0, :, d]),
                             r=['gates'], w=[GK])
                        S.op('dve', lambda e, gv=gv, tt=tt: e.tensor_copy(out=g[:, 32 + tt * 2:34 + tt * 2],
                                                                          in_=gv[:, 1, :, d]), r=['gates'], w=[GK])
                    gp = pb[BC][:, 0:8]
                    S.op('pe', lambda e: e.matmul(gp[:, 0:4], lhsT=TRI, rhs=g[:, 0:4], start=True, stop=True),
                         r=['cst', GK], w=[('pb', BC)])
                    S.op('pe', lambda e: e.matmul(gp[:, 4:8], lhsT=ONES, rhs=g[:, 0:4], start=True, stop=True),
                         r=['cst', GK], w=[('pb', BC)])
                    S.op('act', lambda e: e.copy(out=g[:, 4:12], in_=gp), r=[('pb', BC)], w=[GK])
                    S.op('act', lambda e: e.activation(out=g[:, 12:16], in_=g[:, 4:8], func=AF.Exp), r=[GK], w=[GK])
                    S.op('dve', lambda e: e.tensor_tensor(out=g[:, 16:20], in0=g[:, 8:12], in1=g[:, 4:8],
                                                          op=ALU.subtract), r=[GK], w=[GK])
                    S.op('act', lambda e: e.activation(out=g[:, 16:20], in_=g[:, 16:20], func=AF.Exp), r=[GK], w=[GK])
                    S.op('act', lambda e: e.activation(out=g[:, 20:24], in_=g[:, 8:12], func=AF.Exp), r=[GK], w=[GK])
                    S.op('dve', lambda e: e.tensor_tensor(out=g[:, 24:28], in0=g[:, 12:16], in1=g[:, 32:36],
                                                          op=ALU.mult), r=[GK], w=[GK])
                    S.op('dve', lambda e: e.tensor_scalar(out=g[:, 28:32], in0=g[:, 32:36], scalar1=-1.0,
                                                          scalar2=None, op0=ALU.mult), r=[GK], w=[GK])
                    S.op('pool', lambda e: e.tensor_copy(out=Hgs[:, s0:s0 + 2, 0, :],
                                                         in_=g[:, 12:16].rearrange("p (a h) -> p a h", a=2)),
                         r=[GK], w=hk('Hgs', s0))
                    S.op('pool', lambda e: e.tensor_copy(out=Hgs[:, s0:s0 + 2, 1, :],
                                                         in_=g[:, 20:24].rearrange("p (a h) -> p a h", a=2)),
                         r=[GK], w=hk('Hgs', s0))
                    yield
                    S.op('act', lambda e: e.copy(out=ktok[:], in_=trk), r=[('pb', BA)], w=[K('ktok')])
                    S.op('dve', lambda e: e.tensor_tensor(out=vb[:], in0=trv, in1=bc(g[:, 32:36]), op=ALU.mult),
                         r=[('pb', BB), GK], w=[K('vb')])
                    yield
                    bn = pb[BA][:].rearrange("p (a c) -> p a c", a=4)
                    S.op('dve', lambda e: e.tensor_tensor(out=scr[:], in0=IDENT4, in1=bc(g[:, 4:8]), op=ALU.mult),
                         r=['cst', GK], w=[K('scr')])
                    for j in range(4):
                        S.op('pe', lambda e, j=j: e.matmul(bn[:, j, :], lhsT=ONES, rhs=scr[:, j, :], start=True,
                                                           stop=True), r=['cst', K('scr')], w=[('pb', BA)])
                    yield
                    S.op('dve', lambda e: e.tensor_tensor(out=A1[:], in0=bn, in1=bc(g[:, 4:8]), op=ALU.subtract),
                         r=[('pb', BA), GK], w=[K('A1')])
                    S.op('dve', lambda e: e.tensor_tensor(out=A2[:], in0=A1[:], in1=POS2_4, op=ALU.add),
                         r=[K('A1'), 'cst'], w=[K('A2')])
                    S.op('dve', lambda e: e.tensor_tensor(out=A1[:], in0=A1[:], in1=NEG1_4, op=ALU.add),
                         r=[K('A1'), 'cst'], w=[K('A1')])
                    S.op('act', lambda e: e.activation(out=A1[:], in_=A1[:], func=AF.Exp), r=[K('A1')], w=[K('A1')])
                    S.op('act', lambda e: e.activation(out=A2[:], in_=A2[:], func=AF.Exp, scale=-1.0),
                         r=[K('A2')], w=[K('A2')])
                    yield
                    kk = pb[BB][:].rearrange("p (a c) -> p a c", a=4)
                    zt = pb[BC][:].rearrange("p (a c) -> p a c", a=4)
                    for j in range(4):
                        S.op('pe', lambda e, j=j: e.matmul(kk[:, j, :], lhsT=kTr[:, j, :], rhs=kTr[:, j, :],
                                                           start=True, stop=True), r=[K('kTr')], w=[('pb', BB)])
                    for j in range(4):
                        S.op('pe', lambda e, j=j: e.matmul(zt[:, j, :], lhsT=kTr[:, j, :],
                                                           rhs=HqT[:, s0 + j // 2, j % 2, :], start=True, stop=True),
                             r=[K('kTr')] + hk('HqT', s0), w=[('pb', BC)])
                    yield
                    S.op('dve', lambda e: e.tensor_tensor(out=scr[:], in0=kk, in1=bc(g[:, 28:32]), op=ALU.mult),
                         r=[('pb', BB), GK], w=[K('scr')])
                    S.op('dve', lambda e: e.tensor_tensor(out=scr[:], in0=scr[:], in1=A2[:], op=ALU.mult),
                         r=[K('scr'), K('A2')], w=[K('scr')])
                    S.op('dve', lambda e: e.tensor_tensor(out=h4(Hat, s0), in0=zt, in1=A1[:], op=ALU.mult),
                         r=[('pb', BC), K('A1')], w=hk('Hat', s0))
                    S.op('act', lambda e: e.copy(out=Pm[0][:], in_=scr[:]), r=[K('scr')], w=[K('P0')])
                    yield
                    nt = pb[BA][:].rearrange("p (a c) -> p a c", a=4)
                    for j in range(4):
                        S.op('pe', lambda e, j=j: e.transpose(out=nt[:, j, :], in_=scr[:, j, :], identity=IDENT),
                             r=[K('scr'), 'cst'], w=[('pb', BA)])
                    S.op('act', lambda e: e.copy(out=PTm[0][:], in_=nt), r=[('pb', BA)], w=[K('PT0')])
                    S.op('dve', lambda e: e.tensor_tensor(out=Y[:], in0=nt, in1=IDENT4, op=ALU.add),
                         r=[('pb', BA), 'cst'], w=[K('Y')])
                    yield
                    for k in range(6):
                        a, b = k % 2, (k + 1) % 2
                        p2 = pb[BB][:].rearrange("p (a c) -> p a c", a=4)
                        pt2 = pb[BC][:].rearrange("p (a c) -> p a c", a=4)
                        yu = pb[BA][:].rearrange("p (a c) -> p a c", a=4)
                        for j in range(4):
                            S.op('pe', lambda e, j=j, a=a: e.matmul(p2[:, j, :], lhsT=PTm[a][:, j, :],
                                                                    rhs=Pm[a][:, j, :], start=True, stop=True),
                                 r=[K('PT%d' % a), K('P%d' % a)], w=[('pb', BB)])
                        if k < 5:
                            for j in range(4):
                                S.op('pe', lambda e, j=j, a=a: e.matmul(pt2[:, j, :], lhsT=Pm[a][:, j, :],
                                                                        rhs=PTm[a][:, j, :], start=True, stop=True),
                                     r=[K('PT%d' % a), K('P%d' % a)], w=[('pb', BC)])
                        S.op('act', lambda e, b=b: e.copy(out=Pm[b][:], in_=p2), r=[('pb', BB)], w=[K('P%d' % b)])
                        if k < 5:
                            S.op('dve', lambda e, b=b: e.tensor_copy(out=PTm[b][:], in_=pt2), r=[('pb', BC)],
                                 w=[K('PT%d' % b)])
                        yield
                        for j in range(4):
                            S.op('pe', lambda e, j=j, b=b: e.matmul(yu[:, j, :], lhsT=Pm[b][:, j, :], rhs=Y[:, j, :],
                                                                    start=True, stop=True),
                                 r=[K('P%d' % b), K('Y')], w=[('pb', BA)])
                        S.op('dve', lambda e: e.tensor_tensor(out=Y[:], in0=rd(Y[:]), in1=yu, op=ALU.add),
                             r=[K('Y'), ('pb', BA)], w=[K('Y')])
                        yield
                    up = pb[BB][:].rearrange("p (a c) -> p a c", a=4)
                    wp = pb[BC][:].rearrange("p (a c) -> p a c", a=4)
                    S.op('dve', lambda e: e.tensor_tensor(out=kbg[:], in0=ktok[:], in1=bc(g[:, 24:28]), op=ALU.mult),
                         r=[K('ktok'), GK], w=[K('kbg')])
                    S.op('pool', lambda e: e.tensor_tensor(out=h4(Hkd, s0), in0=ktok[:], in1=bc(g[:, 16:20]),
                                                           op=ALU.mult), r=[K('ktok'), GK], w=hk('Hkd', s0))
                    for j in range(4):
                        S.op('pe', lambda e, j=j: e.matmul(up[:, j, :], lhsT=Y[:, j, :], rhs=vb[:, j, :], start=True,
                                                           stop=True), r=[K('Y'), K('vb')], w=[('pb', BB)])
                    for j in range(4):
                        S.op('pe', lambda e, j=j: e.matmul(wp[:, j, :], lhsT=kbg[:, j, :], rhs=Y[:, j, :], start=True,
                                                           stop=True), r=[K('Y'), K('kbg')], w=[('pb', BC)])
                    S.op('act', lambda e: e.copy(out=h4(Hu, s0), in_=up), r=[('pb', BB)], w=hk('Hu', s0))
                    S.op('act', lambda e: e.copy(out=h4(HwT, s0), in_=wp), r=[('pb', BC)], w=hk('HwT', s0))
                    yield

                def seq(step):
                    n = tile_of(step)
                    s = step % 4
                    i = step % 2
                    second = step >= nts // 2
                    odst = os_[n // 8][(n % 8) * 128:(n % 8 + 1) * 128, :].rearrange("p (h c) -> p h c", h=2)
                    if second:
                        S.op('sp', lambda e: e.dma_start(out=oprev[:], in_=odst), r=[('osend', n)], w=[K('oprev')],
                             dma=pre + 'oprev')
                    q4 = pb[BD][:].rearrange("p (a c) -> p a c", a=4)
                    for h in range(2):
                        S.op('pe', lambda e, h=h: e.matmul(q4[:, h, :], lhsT=HwT[:, s, h, :], rhs=St[:, h, :],
                                                           start=True, stop=True),
                             r=[K('HwT%d' % s), K('S')], w=[('pb', BD)])
                        S.op('pe', lambda e, h=h: e.matmul(q4[:, 2 + h, :], lhsT=HqT[:, s, h, :], rhs=St[:, h, :],
                                                           start=True, stop=True),
                             r=[K('HqT%d' % s), K('S')], w=[('pb', BD)])
                    S.op('dve', lambda e: e.tensor_tensor(out=vnew[:], in0=Hu[:, s], in1=q4[:, 0:2, :],
                                                          op=ALU.subtract),
                         r=[K('Hu%d' % s), ('pb', BD)], w=[K('vnew')])
                    S.op('dve', lambda e: e.tensor_tensor(out=oqs[:], in0=q4[:, 2:4, :], in1=bc(Hgs[:, s, 0, :]),
                                                          op=ALU.mult),
                         r=[('pb', BD), K('Hgs%d' % s)], w=[K('oqs')])
                    yield
                    for h in range(2):
                        S.op('pe', lambda e, h=h: e.matmul(q4[:, h, :], lhsT=Hat[:, s, h, :], rhs=vnew[:, h, :],
                                                           start=True, stop=True),
                             r=[K('Hat%d' % s), K('vnew')], w=[('pb', BD)])
                        S.op('pe', lambda e, h=h: e.matmul(q4[:, 2 + h, :], lhsT=Hkd[:, s, h, :], rhs=vnew[:, h, :],
                                                           start=True, stop=True),
                             r=[K('Hkd%d' % s), K('vnew')], w=[('pb', BD)])
                    oti = ot[i]
                    S.op('dve', lambda e: e.tensor_tensor(out=oti[:], in0=oqs[:], in1=q4[:, 0:2, :], op=ALU.add),
                         r=[K('oqs'), ('pb', BD)], w=[K('ot%d' % i)])
                    if Sm is None:
                        for h in range(2):
                            S.op('dve', lambda e, h=h: e.scalar_tensor_tensor(
                                out=St[:, h, :], in0=St[:, h, :].bitcast(F32), scalar=Hgs[:, s, 1, h:h + 1],
                                in1=q4[:, 2 + h, :], op0=ALU.mult, op1=ALU.add),
                                r=[K('S'), K('Hgs%d' % s), ('pb', BD)], w=[K('S')])
                    else:
                        for h in range(2):
                            S.op('dve', lambda e, h=h: e.scalar_tensor_tensor(
                                out=Sm[:, h, :], in0=Sm[:, h, :], scalar=Hgs[:, s, 1, h:h + 1],
                                in1=q4[:, 2 + h, :], op0=ALU.mult, op1=ALU.add),
                                r=[K('Sm'), K('Hgs%d' % s), ('pb', BD)], w=[K('Sm')])
                        S.op('act', lambda e: e.copy(out=St[:], in_=Sm[:]), r=[K('Sm')], w=[K('S')])
                    if second:
                        S.op('pool', lambda e: e.tensor_tensor(out=oti[:], in0=oti[:], in1=oprev[:], op=ALU.add),
                             r=[K('ot%d' % i), K('oprev')], w=[K('ot%d' % i)])
                    S.op('sp', lambda e: e.dma_start(out=odst, in_=oti[:]), r=[K('ot%d' % i)], w=[('osend', n)],
                         dma=pre + 'ot%d' % i)
                    yield

                def chain2(*gs_):
                    for g_ in gs_:
                        for _ in g_:
                            yield

                load(0)
                if npairs > 1:
                    load(1)
                yield
                for _ in prep(0):
                    yield
                for p in range(npairs):
                    if p + 2 < npairs:
                        load(p + 2)
                    A = prep(p + 1) if p + 1 < npairs else iter(())
                    B = chain2(seq(2 * p), seq(2 * p + 1))
                    a_alive, b_alive, cnt = True, True, 0
                    while a_alive or b_alive:
                        cnt += 1
                        if a_alive:
                            try:
                                next(A)
                            except StopIteration:
                                a_alive = False
                            yield
                        if b_alive and (cnt % 5 == 0 or not a_alive):
                            try:
                                next(B)
                            except StopIteration:
                                b_alive = False
                            yield

            gens = [dir_gen(0), dir_gen(1)]
            alive = [True, True]
            while any(alive):
                for gi, g in enumerate(gens):
                    if alive[gi]:
                        try:
                            next(g)
                        except StopIteration:
                            alive[gi] = False

    if test is not None:
        gates_t = sb("gates_t", [128, test, 8], F32)
        S.op('sp', lambda e: e.dma_start(out=gates_t[:], in_=gates_in), w=['gates'], dma='c1')
        os_t = [o_out[i * 1024:(i + 1) * 1024, :] for i in range(max(1, test // 8))]
        run_d2(gates_t, qkv_in, os_t, test)
        S.barrier()
        S.emit()
        stack.close()
        return nc

    for l in range(depth):
        x_src = x_in if l == 0 else x_res
        with ExitStack() as st:
            xin = [sb("xin%d" % i, [128, D], F32, st) for i in range(2)]
            xbf = [sb("xbf%d" % i, [128, D], BF16, st) for i in range(2)]
            junk = sb("junk", [128, D], BF16, st)
            ssq = sb("ssq", [128, NT], F32, st)
            rtmp = sb("rtmp", [128, NT], F32, st)
            rstd = sb("rstd", [128, NT], F32, st)
            for t in range(NT):
                i = t % 2
                S.op('sp', lambda e, t=t, i=i: e.dma_start(out=xin[i][:], in_=x_src[t * 128:(t + 1) * 128, :]),
                     r=[('xres', t)], w=[('xin', i)], dma='xin%d' % i)
                S.op('act', lambda e, t=t, i=i: e.activation(out=junk[:], in_=xin[i][:], func=AF.Square,
                                                             accum_out=ssq[:, t:t + 1]),
                     r=[('xin', i)], w=['junk', ('ssq', t)])
                rsqrt_chain(ssq[:, t:t + 1], rstd[:, t:t + 1], 1.0 / D, [('ssq', t)], ('rstd', t),
                            rtmp[:, t:t + 1], ('rtmp', t))
                S.op('dve', lambda e, t=t, i=i: e.tensor_scalar(out=xbf[i][:], in0=xin[i][:], scalar1=rstd[:, t:t + 1],
                                                                scalar2=None, op0=ALU.mult),
                     r=[('xin', i), ('rstd', t)], w=[('xbf', i)])
                banks = (0, 1) if i == 0 else (2, 3)
                for kc in range(KC):
                    bk = banks[kc // 8]
                    pv = pb[bk][:].bitcast(BF16)
                    S.op('pe', lambda e, kc=kc, i=i, pv=pv: e.transpose(
                        out=pv[:, (kc % 8) * 128:(kc % 8 + 1) * 128], in_=xbf[i][:, kc * 128:(kc + 1) * 128],
                        identity=identb[:]),
                        r=[('xbf', i), 'identb'], w=[('pb', bk)])
                for hb, eng in ((0, 'act'), (1, 'dve')):
                    bk = banks[hb]
                    pv = pb[bk][:].bitcast(BF16).rearrange("p (k t) -> p k t", k=8)
                    dst = xT[:, hb * 8:(hb + 1) * 8, t * 128:(t + 1) * 128]
                    if eng == 'act':
                        S.op('act', lambda e, pv=pv, dst=dst: e.copy(out=dst, in_=pv), r=[('pb', bk)], w=[('xT', t)])
                    else:
                        S.op('dve', lambda e, pv=pv, dst=dst: e.tensor_copy(out=dst, in_=pv), r=[('pb', bk)],
                             w=[('xT', t)])
                if t % 4 == 3:
                    tb = t // 4
                    for half in range(2):
                        ci = tb * 2 + half
                        S.op('sp', lambda e, tb=tb, ci=ci, half=half: e.dma_start(
                            out=hs[ci].rearrange("(p k) t -> p k t", k=KC),
                            in_=xT[half * 64:(half + 1) * 64, :, tb * 512:(tb + 1) * 512]),
                            r=[('xT', tt) for tt in range(tb * 4, tb * 4 + 4)], w=[('hs', ci)], dma='hs%d' % ci)
                        S.op('pool', lambda e, ci=ci: e.collective_compute("AllGather", ALU.bypass, replica_groups=RG,
                                                                           ins=[hs[ci].opt()], outs=[ha[ci].opt()]),
                             r=[('hs', ci)], w=[('ha', ci)], cc='ag1')
        if STOP < 2:
            continue
        S.barrier()
        if STOP < 3:
            continue

        with ExitStack() as st2:
            gates = sb("gates", [128, NTS, 8], F32, st2)
            with ExitStack() as st:
                wst = [sb("wst%d" % i, [128, KC, 128], F32, st) for i in range(2)]
                wdnb = sb("wdnb", [128, 6, KC, 128], BF16, st)
                wgs = sb("wgs", [128, KC, 8], F32, st)
                wgb = sb("wgb", [128, KC, 8], BF16, st)
                nwt = sb("nwt", [128, KC], F32, st)
                cwt = sb("cwt", [128, 6, 5], F32, st)
                dg = sb("dg", [128, 30, 128], RT, st)
                onesr = sb("onesr", [128, 128], RT, st)
                alt = sb("alt", [128, 4], F32, st)
                dtt = sb("dtt", [128, 4], F32, st)
                nega = sb("nega", [128, 4], F32, st)
                hblk = [sb("hblk%d" % i, [128, KC, 512], BF16, st) for i in range(2)]
                PB = [sb("PB%d" % i, [128, 6, 516], RT, st) for i in range(2)]
                cv = [sb("cv%d" % i, [128, 512], F32, st) for i in range(2)]
                sq = [sb("sq%d" % i, [128, 512], RT, st) for i in range(2)]
                rn = [sb("rn%d" % i, [128, 512], F32, st) for i in range(2)]
                qo = [sb("qo%d" % i, [128, 512], F32, st) for i in range(2)]
                gt = sb("gt", [128, 4, 8], F32, st)
                S.op('sp', lambda e: e.dma_start(out=nwt[:], in_=normw[l]), w=['nwt'], dma='c0')
                S.op('sp', lambda e: e.dma_start(out=cwt[:], in_=convw[l]), w=['cwt'], dma='c1')
                S.op('sp', lambda e: e.dma_start(out=alt[:], in_=alog[l]), w=['alt'], dma='c2')
                S.op('sp', lambda e: e.dma_start(out=dtt[:], in_=dtb[l]), w=['dtt'], dma='c3')
                S.op('sp', lambda e: e.dma_start(out=wgs[:], in_=wdg[l]), w=['wgs'], dma='c4')
                for c in range(6):
                    i = c % 2
                    S.op('sp', lambda e, c=c, i=i: e.dma_start(out=wst[i][:], in_=wdn[l, c]), w=[('wst', i)],
                         dma='wst%d' % i)
                    S.op('pool', lambda e, c=c, i=i: e.tensor_tensor(
                        out=wdnb[:, c], in0=wst[i][:], in1=nwt[:].unsqueeze(2).broadcast_to([128, KC, 128]),
                        op=ALU.mult), r=[('wst', i), 'nwt'], w=[('wdnb', c)])
                S.op('pool', lambda e: e.tensor_tensor(out=wgb[:], in0=wgs[:],
                                                       in1=nwt[:].unsqueeze(2).broadcast_to([128, KC, 8]),
                                                       op=ALU.mult), r=['wgs', 'nwt'], w=['wgb'])
                for c in range(6):
                    for tau in range(5):
                        S.op('dve', lambda e, c=c, tau=tau: e.tensor_scalar(
                            out=dg[:, c * 5 + tau, :], in0=IDENT, scalar1=cwt[:, c, tau:tau + 1], scalar2=None,
                            op0=ALU.mult), r=['cst', 'cwt'], w=[('dg', c)])
                S.op('dve', lambda e: e.tensor_copy(out=onesr[:], in_=ONES), r=['cst'], w=['onesr'])
                S.op('act', lambda e: e.activation(out=nega[:], in_=alt[:], func=AF.Exp), r=['alt'], w=['nega'])
                S.op('dve', lambda e: e.tensor_scalar(out=nega[:], in0=nega[:], scalar1=-1.0, scalar2=None,
                                                      op0=ALU.mult), r=['nega'], w=['nega'])

                def conv_block(jb):
                    P = PB[jb % 2]
                    for c in range(6):
                        i2 = c % 2
                        bk = 5 + i2
                        for tau in range(5):
                            S.op('pe', lambda e, c=c, tau=tau, P=P, bk=bk: e.matmul(
                                pb[bk][:], lhsT=dg[:, c * 5 + tau, :], rhs=P[:, c, tau:tau + 512],
                                start=(tau == 0), stop=(tau == 4)),
                                r=[('dg', c), ('PB', jb % 2)], w=[('pb', bk)])
                        S.op('act', lambda e, i2=i2, bk=bk: e.activation(out=cv[i2][:], in_=pb[bk][:], func=AF.Silu),
                             r=[('pb', bk)], w=[('cv', i2)])
                        dst = qkv_s[c * 128:(c + 1) * 128, jb * 512:(jb + 1) * 512]
                        if c % 3 == 2:
                            S.op('sp', lambda e, i2=i2, dst=dst: e.dma_start(out=dst, in_=cv[i2][:]),
                                 r=[('cv', i2)], w=[('qkv', c, jb)], dma='qs%d' % i2)
                        else:
                            S.op('pool', lambda e, i2=i2: e.tensor_tensor(out=sq[i2][:], in0=cv[i2][:], in1=cv[i2][:],
                                                                         op=ALU.mult),
                                 r=[('cv', i2)], w=[('sq', i2)])
                            S.op('pe', lambda e, i2=i2: e.matmul(pb[7][:], lhsT=onesr[:], rhs=sq[i2][:],
                                                                 start=True, stop=True),
                                 r=['onesr', ('sq', i2)], w=[('pb', 7)])
                            S.op('act', lambda e, i2=i2: e.activation(out=rn[i2][:], in_=pb[7][:], func=AF.Sqrt,
                                                                      bias=EPS),
                                 r=[('pb', 7)], w=[('rn', i2)])
                            S.op('dve', lambda e, i2=i2: e.reciprocal(out=rn[i2][:], in_=rn[i2][:]),
                                 r=[('rn', i2)], w=[('rn', i2)])
                            sc = (128.0 ** -0.5) if c % 3 == 0 else 1.0
                            S.op('dve', lambda e, i2=i2, sc=sc: e.scalar_tensor_tensor(
                                out=qo[i2][:], in0=cv[i2][:], scalar=sc, in1=rn[i2][:], op0=ALU.mult, op1=ALU.mult),
                                r=[('cv', i2), ('rn', i2)], w=[('qo', i2)])
                            S.op('sp', lambda e, i2=i2, dst=dst: e.dma_start(out=dst, in_=qo[i2][:]),
                                 r=[('qo', i2)], w=[('qkv', c, jb)], dma='qo%d' % i2)

                for jb in range(16):
                    hi = jb % 2
                    r_, off = jb // 4, (jb % 4) * 512
                    for half in range(2):
                        ci = (jb % 4) * 2 + half
                        src = ha[ci][r_ * 64 * KC:(r_ + 1) * 64 * KC, :].rearrange("(p k) t -> p k t", k=KC)
                        S.op('sp', lambda e, hi=hi, src=src, half=half: e.dma_start(
                            out=hblk[hi][half * 64:(half + 1) * 64, :, :], in_=src),
                            r=[('ha', ci)], w=[('hblk', hi)], dma='hblk%d' % hi)
                    P = PB[jb % 2]
                    for c in range(6):
                        bk = c % 4
                        for kc in range(KC):
                            S.op('pe', lambda e, c=c, kc=kc, bk=bk, hi=hi: e.matmul(
                                pb[bk][:], lhsT=wdnb[:, c, kc, :], rhs=hblk[hi][:, kc, :],
                                start=(kc == 0), stop=(kc == KC - 1)),
                                r=[('wdnb', c), ('hblk', hi)], w=[('pb', bk)])
                        S.op('act', lambda e, c=c, bk=bk, P=P: e.copy(out=P[:, c, 2:514], in_=pb[bk][:]),
                             r=[('pb', bk)], w=[('PB', jb % 2)])
                    for k in range(4):
                        for kc in range(KC):
                            S.op('pe', lambda e, k=k, kc=kc, hi=hi: e.matmul(
                                pb[4][:, k * 8:(k + 1) * 8], lhsT=hblk[hi][:, kc, k * 128:(k + 1) * 128],
                                rhs=wgb[:, kc, :], start=(kc == 0), stop=(kc == KC - 1)),
                                r=['wgb', ('hblk', hi)], w=[('pb', 4)])
                    gsl = gates[:, jb * 4:(jb + 1) * 4, :]
                    pg = pb[4][:, 0:32].rearrange("p (k g) -> p k g", k=4)
                    S.op('dve', lambda e, pg=pg: e.tensor_tensor(
                        out=gt[:, :, 0:4], in0=pg[:, :, 0:4], in1=dtt[:].unsqueeze(1).broadcast_to([128, 4, 4]),
                        op=ALU.add), r=[('pb', 4), 'dtt'], w=['gt'])
                    S.op('act', lambda e: e.activation(out=gt[:, :, 0:4], in_=gt[:, :, 0:4], func=AF.Exp),
                         r=['gt'], w=['gt'])
                    S.op('act', lambda e: e.activation(out=gt[:, :, 0:4], in_=gt[:, :, 0:4], func=AF.Ln, bias=1.0),
                         r=['gt'], w=['gt'])
                    S.op('dve', lambda e, gsl=gsl: e.tensor_tensor(
                        out=gsl[:, :, 0:4], in0=gt[:, :, 0:4], in1=nega[:].unsqueeze(1).broadcast_to([128, 4, 4]),
                        op=ALU.mult), r=['gt', 'nega'], w=['gates'])
                    S.op('act', lambda e, gsl=gsl, pg=pg: e.activation(out=gsl[:, :, 4:8], in_=pg[:, :, 4:8],
                                                                       func=AF.Sigmoid),
                         r=[('pb', 4)], w=['gates'])
                    if jb > 0:
                        Pp = PB[(jb - 1) % 2]
                        S.op('pool', lambda e, P=P, Pp=Pp: e.tensor_copy(out=Pp[:, :, 514:516],
                                                                         in_=(P[:, :, 2:4].bitcast(F32) if RT == F32R else P[:, :, 2:4])),
                             r=[('PB', jb % 2)], w=[('PB', (jb - 1) % 2)])
                        S.op('pool', lambda e, P=P, Pp=Pp: e.tensor_copy(out=P[:, :, 0:2],
                                                                         in_=(Pp[:, :, 512:514].bitcast(F32) if RT == F32R else Pp[:, :, 512:514])),
                             r=[('PB', (jb - 1) % 2)], w=[('PB', jb % 2)])
                        conv_block(jb - 1)
                    else:
                        S.op('dve', lambda e, P=P: e.tensor_scalar(out=P[:, :, 0:2], in0=cwt[:, :, 0:2], scalar1=0.0,
                                                                   scalar2=None, op0=ALU.mult), r=['cwt'], w=[('PB', 0)])
                Pl = PB[15 % 2]
                S.op('dve', lambda e, Pl=Pl: e.tensor_scalar(out=Pl[:, :, 514:516], in0=cwt[:, :, 0:2], scalar1=0.0,
                                                             scalar2=None, op0=ALU.mult), r=['cwt'], w=[('PB', 15 % 2)])
                conv_block(15)
            S.barrier()
            if STOP < 4:
                continue

            run_d2(gates, qkv_s, os_, NTS)
            if DBG and l == 0:
                S.barrier()
                S.op('sp', lambda e: e.dma_start(out=dbg_g, in_=gates[:]), r=['gates'], w=['dbg_g'], dma='dbg')
                S.op('sp', lambda e: e.dma_start(out=dbg_qkv, in_=qkv_s), w=['dbg_qkv'], dma='dbg')
                for ci in range(8):
                    S.op('sp', lambda e, ci=ci: e.dma_start(out=dbg_o[ci * 1024:(ci + 1) * 1024, :], in_=os_[ci]),
                         w=[('dbg_o', ci)], dma='dbg')
            S.barrier()
        if STOP < 5:
            continue
        for ci in range(8):
            S.op('pool', lambda e, ci=ci: e.collective_compute("AllGather", ALU.bypass, replica_groups=RG,
                                                               ins=[os_[ci].opt()], outs=[oa[ci].opt()]),
                 r=[('osend', n) for n in range(ci * 8, ci * 8 + 8)], w=[('oa', ci)], cc='ag2')

        if STOP < 6:
            continue
        with ExitStack() as st:
            vy = sb("vy", [128, NT, 1024], BF16, st)
            yaT = sb("yaT", [128, 8, T], BF16, st)
            wst = [sb("wst%d" % i, [128, KC, 128], F32, st) for i in range(2)]
            wbf = [sb("wbf%d" % i, [128, KC, 128], BF16, st) for i in range(4)]
            nwt = sb("nwt", [128, KC], F32, st)
            swT_s = sb("swT_s", [128, 8, 128], F32, st)
            swT = sb("swT", [128, 8, 128], BF16, st)
            gu = [sb("gu%d" % i, [128, 512], F32, st) for i in range(2)]
            sz = [sb("sz%d" % i, [128, 512], F32, st) for i in range(2)]
            t2 = [sb("t2_%d" % i, [128, 512], F32, st) for i in range(2)]
            st6 = sb("st6", [128, NT, 8, 6], F32, st)
            mv = sb("mv", [128, NT, 2], F32, st)
            lrs = sb("lrs", [128, NT], F32, st)
            ltmp = sb("ltmp", [128, NT], F32, st)
            sgubt = sb("sgubt", [128, 8, 128], F32, st)
            S.op('sp', lambda e: e.dma_start(out=nwt[:], in_=normw[l]), w=['nwt'], dma='c0')
            S.op('sp', lambda e: e.dma_start(out=swT_s[:], in_=sguwT[l]), w=['swT_s'], dma='c1')
            S.op('sp', lambda e: e.dma_start(out=sgubt[:], in_=sgub[l]), w=['sgubt'], dma='c2')
            S.op('dve', lambda e: e.tensor_copy(out=swT[:], in_=swT_s[:]), r=['swT_s'], w=['swT'])

            wq = []

            def wload(bi):
                i = bi % 2
                j = bi % 4
                S.op('sp', lambda e, bi=bi, i=i: e.dma_start(out=wst[i][:], in_=wsg[l, bi]), w=[('wst', i)],
                     dma='wst%d' % i)
                S.op('pool', lambda e, i=i, j=j: e.tensor_tensor(
                    out=wbf[j][:], in0=wst[i][:], in1=nwt[:].unsqueeze(2).broadcast_to([128, KC, 128]),
                    op=ALU.mult), r=[('wst', i), 'nwt'], w=[('wbf', j)])

            def proj(bi, tb, bk):
                j = bi % 4
                for kc in range(KC):
                    S.op('pe', lambda e, kc=kc, j=j, tb=tb, bk=bk: e.matmul(
                        pb[bk][:], lhsT=wbf[j][:, kc, :], rhs=xT[:, kc, tb * 512:(tb + 1) * 512],
                        start=(kc == 0), stop=(kc == KC - 1)),
                        r=[('wbf', j)] + [('xT', tt) for tt in range(tb * 4, tb * 4 + 4)], w=[('pb', bk)])

            wload(0)
            wload(1)
            with ExitStack() as stv:
                lngt = sb("lngt", [128, 1024], F32, stv)
                lnbt = sb("lnbt", [128, 1024], F32, stv)
                S.op('sp', lambda e: e.dma_start(out=lngt[:], in_=lng[l]), w=['lngt'], dma='c3')
                S.op('sp', lambda e: e.dma_start(out=lnbt[:], in_=lnb[l]), w=['lnbt'], dma='c4')
                cnt = 0
                for vbk in range(8):
                    if vbk + 2 < 32:
                        wload(vbk + 2)
                    for tb in range(4):
                        bk = cnt % 4
                        gi = cnt % 2
                        cnt += 1
                        proj(vbk, tb, bk)
                        S.op('act', lambda e, bk=bk, gi=gi: e.activation(out=gu[gi][:], in_=pb[bk][:], func=AF.Gelu),
                             r=[('pb', bk)], w=[('gu', gi)])
                        tbk = 4 + gi
                        trv = pb[tbk][:].rearrange("p (a c) -> p a c", a=4)
                        for k in range(4):
                            S.op('pe', lambda e, k=k, gi=gi, trv=trv: e.transpose(
                                out=trv[:, k, :], in_=gu[gi][:, k * 128:(k + 1) * 128], identity=IDENT),
                                r=[('gu', gi), 'cst'], w=[('pb', tbk)])
                        for k in range(4):
                            t = tb * 4 + k
                            S.op('dve', lambda e, k=k, t=t, vbk=vbk, trv=trv: e.bn_stats(out=st6[:, t, vbk, :],
                                                                                         in_=trv[:, k, :]),
                                 r=[('pb', tbk)], w=[('st6', t)])
                        S.op('act', lambda e, tb=tb, vbk=vbk, trv=trv: e.copy(
                            out=vy[:, tb * 4:(tb + 1) * 4, vbk * 128:(vbk + 1) * 128], in_=trv),
                            r=[('pb', tbk)], w=[('vy', tt) for tt in range(tb * 4, tb * 4 + 4)])
                for t in range(NT):
                    S.op('dve', lambda e, t=t: e.bn_aggr(out=mv[:, t, :],
                                                         in_=st6[:, t].rearrange("p a b -> p (a b)")),
                         r=[('st6', t)], w=[('mv', t)])
                    rsqrt_chain(mv[:, t, 1:2], lrs[:, t:t + 1], 1.0, [('mv', t)], ('lrs', t), ltmp[:, t:t + 1],
                                ('ltmp', t))
                    S.op('dve', lambda e, t=t: e.tensor_scalar(out=vy[:, t, :], in0=vy[:, t, :], scalar1=mv[:, t, 0:1],
                                                               scalar2=lrs[:, t:t + 1], op0=ALU.subtract,
                                                               op1=ALU.mult),
                         r=[('vy', t), ('mv', t), ('lrs', t)], w=[('vy', t)])
                    S.op('pool', lambda e, t=t: e.tensor_tensor(out=vy[:, t, :], in0=vy[:, t, :], in1=lngt[:],
                                                                op=ALU.mult), r=[('vy', t), 'lngt'], w=[('vy', t)])
                    S.op('pool', lambda e, t=t: e.tensor_tensor(out=vy[:, t, :], in0=vy[:, t, :], in1=lnbt[:],
                                                                op=ALU.add), r=[('vy', t), 'lnbt'], w=[('vy', t)])
                if DBG and l == 0:
                    S.op('sp', lambda e: e.dma_start(out=dbg_vn, in_=vy[:]), r=[('vy', t) for t in range(NT)],
                         w=['dbg_vn'], dma='dbg')
                cnt = 0
                for g in range(8):
                    bu, bz = 8 + 2 * g, 9 + 2 * g
                    for nb in (bu + 2, bz + 2):
                        if nb < 32:
                            wload(nb)
                    for tb in range(4):
                        gi = cnt % 2
                        cnt += 1
                        proj(bu, tb, 0 + gi)
                        proj(bz, tb, 2 + gi)
                        S.op('act', lambda e, gi=gi: e.activation(out=gu[gi][:], in_=pb[0 + gi][:], func=AF.Gelu),
                             r=[('pb', 0 + gi)], w=[('gu', gi)])
                        S.op('act', lambda e, gi=gi: e.activation(out=sz[gi][:], in_=pb[2 + gi][:], func=AF.Silu),
                             r=[('pb', 2 + gi)], w=[('sz', gi)])
                        spb = 6 + gi
                        for k in range(4):
                            t = tb * 4 + k
                            S.op('pe', lambda e, k=k, t=t, g=g, spb=spb: e.matmul(
                                pb[spb][:, k * 128:(k + 1) * 128], lhsT=vy[:, t, g * 128:(g + 1) * 128],
                                rhs=swT[:, g, :], start=True, stop=True),
                                r=[('vy', t), 'swT'], w=[('pb', spb)])
                        S.op('pool', lambda e, gi=gi: e.tensor_tensor(out=gu[gi][:], in0=gu[gi][:], in1=sz[gi][:],
                                                                      op=ALU.mult),
                             r=[('gu', gi), ('sz', gi)], w=[('gu', gi)])
                        S.op('dve', lambda e, gi=gi, g=g, spb=spb: e.tensor_tensor(
                            out=t2[gi][:].rearrange("p (a c) -> p a c", a=4),
                            in0=pb[spb][:].rearrange("p (a c) -> p a c", a=4),
                            in1=sgubt[:, g, :].unsqueeze(1).broadcast_to([128, 4, 128]), op=ALU.add),
                             r=[('pb', spb), 'sgubt'], w=[('t2', gi)])
                        S.op('dve', lambda e, gi=gi, g=g, tb=tb: e.tensor_tensor(
                            out=yaT[:, g, tb * 512:(tb + 1) * 512], in0=t2[gi][:], in1=gu[gi][:], op=ALU.mult),
                            r=[('t2', gi), ('gu', gi)], w=[('yaT', g)])
            S.barrier()
            onT = vy[:].rearrange("p a b -> p (a b)").rearrange("p (j t) -> p j t", j=8)
            with ExitStack() as sto:
                og = [sb("og%d" % i, [128, 1024], F32, sto) for i in range(2)]
                oc = sb("oc", [128, 1024], F32, sto)
                ojk = sb("ojk", [128, 128], BF16, sto)
                oss = sb("oss", [128, 8], F32, sto)
                otmp = sb("otmp", [128, 8], F32, sto)
                ors = sb("ors", [128, 8], F32, sto)
                ocnt = 0
                for t in range(NT):
                    for q in range(4):
                        oi = ocnt % 2
                        ocnt += 1
                        oci = 2 * q + t // 8
                        src = oa[oci].rearrange("(r k) c -> k r c", r=4)[(t % 8) * 128:(t % 8 + 1) * 128]
                        S.op('sp', lambda e, oi=oi, src=src: e.dma_start(
                            out=og[oi][:].rearrange("p (r c) -> p r c", r=4), in_=src),
                            r=[('oa', oci)], w=[('og', oi)], dma='og%d' % oi)
                        if q == 0:
                            S.op('dve', lambda e, oi=oi: e.tensor_scalar(out=oc[:], in0=og[oi][:], scalar1=selt[:, 0:1],
                                                                         scalar2=None, op0=ALU.mult),
                                 r=[('og', oi), 'selt'], w=['oc'])
                        else:
                            S.op('dve', lambda e, q=q, oi=oi: e.scalar_tensor_tensor(
                                out=oc[:], in0=og[oi][:], scalar=selt[:, q:q + 1], in1=oc[:], op0=ALU.mult,
                                op1=ALU.add), r=[('og', oi), 'selt', 'oc'], w=['oc'])
                    for h in range(8):
                        S.op('act', lambda e, h=h: e.activation(out=ojk[:], in_=oc[:, h * 128:(h + 1) * 128],
                                                                func=AF.Square, accum_out=oss[:, h:h + 1]),
                             r=['oc'], w=['ojk', 'oss'])
                    rsqrt_chain(oss[:], ors[:], 1.0 / 128, ['oss'], 'ors', otmp[:], 'otmp')
                    S.op('dve', lambda e: e.tensor_tensor(
                        out=oc[:].rearrange("p (h c) -> p h c", h=8), in0=oc[:].rearrange("p (h c) -> p h c", h=8),
                        in1=ors[:].unsqueeze(2).broadcast_to([128, 8, 128]), op=ALU.mult),
                        r=['oc', 'ors'], w=['oc'])
                    for hb in range(2):
                        bk = 2 + hb
                        trv = pb[bk][:].rearrange("p (a c) -> p a c", a=4)
                        for k in range(4):
                            j = hb * 4 + k
                            S.op('pe', lambda e, k=k, j=j, trv=trv: e.transpose(
                                out=trv[:, k, :], in_=oc[:, j * 128:(j + 1) * 128], identity=IDENT),
                                r=['oc', 'cst'], w=[('pb', bk)])
                        S.op('act', lambda e, hb=hb, t=t, trv=trv: e.copy(
                            out=onT[:, hb * 4:(hb + 1) * 4, t * 128:(t + 1) * 128], in_=trv),
                            r=[('pb', bk)], w=[('onT', t)])
            cnt = 0
            for j in range(8):
                bi = 24 + j
                if bi + 2 < 32:
                    wload(bi + 2)
                for tb in range(4):
                    gi = cnt % 2
                    cnt += 1
                    proj(bi, tb, 0 + gi)
                    S.op('act', lambda e, gi=gi: e.activation(out=sz[gi][:], in_=pb[0 + gi][:], func=AF.Silu),
                         r=[('pb', 0 + gi)], w=[('sz', gi)])
                    S.op('dve', lambda e, gi=gi, j=j, tb=tb: e.scalar_tensor_tensor(
                        out=onT[:, j, tb * 512:(tb + 1) * 512], in0=onT[:, j, tb * 512:(tb + 1) * 512],
                        scalar=dnwt[:, l:l + 1], in1=sz[gi][:], op0=ALU.mult, op1=ALU.mult),
                        r=[('onT', tt) for tt in range(tb * 4, tb * 4 + 4)] + [('sz', gi), 'dnwt'],
                        w=[('ybT', j, tb)])
            S.barrier()
            if DBG and l == 0:
                S.op('sp', lambda e: e.dma_start(out=dbg_ya, in_=yaT[:]), w=['dbg_ya'], dma='dbg')
                S.op('sp', lambda e: e.dma_start(out=dbg_yb, in_=onT), w=['dbg_yb'], dma='dbg')
                S.barrier()
            xTf = xT[:].rearrange("p a b -> p (a b)").bitcast(F32)
            wos = xTf[:, 0:8192].rearrange("p (k c) -> p k c", k=KC)
            wob = xT[:].rearrange("p a b -> p (a b)")[:, 16384:16384 + 8192].rearrange("p (k c) -> p k c", k=KC)
            with ExitStack() as sto:
                xo = [sb("xo%d" % i, [128, 512], F32, sto) for i in range(2)]
                xn = [sb("xn%d" % i, [128, 512], F32, sto) for i in range(2)]
                cnt = 0
                for jb in range(4):
                    S.op('sp', lambda e, jb=jb: e.dma_start(out=wos, in_=wout[l, jb]), w=['wos'] + ALLXT, dma='wos')
                    S.op('pool', lambda e: e.tensor_copy(out=wob, in_=wos), r=['wos'], w=['wob'] + ALLXT)
                    for t in range(NT):
                        i = cnt % 2
                        bk = 4 + cnt % 4
                        cnt += 1
                        S.op('sp', lambda e, t=t, jb=jb, i=i: e.dma_start(
                            out=xo[i][:], in_=x_src[t * 128:(t + 1) * 128, jb * 512:(jb + 1) * 512]),
                            r=[('xres', t)], w=[('xo', i)], dma='xo%d' % i)
                        for c in range(16):
                            lhs = yaT[:, c, t * 128:(t + 1) * 128] if c < 8 else onT[:, c - 8, t * 128:(t + 1) * 128]
                            rk = [('yaT', c)] if c < 8 else [('ybT', c - 8, t // 4)]
                            S.op('pe', lambda e, c=c, lhs=lhs, bk=bk: e.matmul(pb[bk][:], lhsT=lhs, rhs=wob[:, c, :],
                                                                              start=(c == 0), stop=(c == 15)),
                                 r=rk + ['wob'], w=[('pb', bk)])
                        S.op('dve', lambda e, i=i, bk=bk: e.tensor_tensor(out=xn[i][:], in0=pb[bk][:], in1=xo[i][:],
                                                                          op=ALU.add),
                             r=[('pb', bk), ('xo', i)], w=[('xn', i)])
                        S.op('sp', lambda e, t=t, jb=jb, i=i: e.dma_start(
                            out=x_res[t * 128:(t + 1) * 128, jb * 512:(jb + 1) * 512], in_=xn[i][:]),
                            r=[('xn', i)], w=[('xres2', t, jb)], dma='xn%d' % i)
            for t in range(NT):
                S.state[('xres', t)] = [None, {}]
            S.barrier()
        S.new_epoch()

    with ExitStack() as st:
        xin = [sb("fxin%d" % i, [128, D], F32, st) for i in range(2)]
        xot = [sb("fxo%d" % i, [128, D], F32, st) for i in range(2)]
        junk = sb("fjunk", [128, D], BF16, st)
        fnt = sb("fnt", [128, D], F32, st)
        ssq = sb("fssq", [128, NT], F32, st)
        rtmp = sb("frtmp", [128, NT], F32, st)
        rstd = sb("frstd", [128, NT], F32, st)
        S.op('sp', lambda e: e.dma_start(out=fnt[:], in_=fnw), w=['fnt'], dma='c0')
        x_src = x_res if depth > 0 else x_in
        otoks = []
        for t in range(NT):
            i = t % 2
            S.op('sp', lambda e, t=t, i=i: e.dma_start(out=xin[i][:], in_=x_src[t * 128:(t + 1) * 128, :]),
                 w=[('fxin', i)], dma='fxin%d' % i)
            S.op('act', lambda e, t=t, i=i: e.activation(out=junk[:], in_=xin[i][:], func=AF.Square,
                                                         accum_out=ssq[:, t:t + 1]),
                 r=[('fxin', i)], w=['fjunk', ('fssq', t)])
            rsqrt_chain(ssq[:, t:t + 1], rstd[:, t:t + 1], 1.0 / D, [('fssq', t)], ('frstd', t), rtmp[:, t:t + 1],
                        ('frtmp', t))
            S.op('dve', lambda e, t=t, i=i: e.scalar_tensor_tensor(out=xot[i][:], in0=xin[i][:],
                                                                   scalar=rstd[:, t:t + 1], in1=fnt[:],
                                                                   op0=ALU.mult, op1=ALU.mult),
                 r=[('fxin', i), ('frstd', t), 'fnt'], w=[('fxo', i)])
            otoks.append(S.op('sp', lambda e, t=t, i=i: e.dma_start(out=out[t * 128:(t + 1) * 128, :], in_=xot[i][:]),
                              r=[('fxo', i)], w=[('out', t)], dma='fxo%d' % i))
    S.barrier()
    S.emit()
    stack.close()
    return nc


def _consts():
    c = np.zeros((128, 8, 128), np.float32)
    p = np.arange(128)[:, None]
    f = np.arange(128)[None, :]
    c[:, C_ID] = (p == f)
    c[:, C_ONE] = 1.0
    c[:, C_TRF] = (p <= f)
    c[:, C_TRB] = (p >= f)
    c[:, C_N1F] = np.where(f >= p, 0.0, -BIG)
    c[:, C_N1B] = np.where(f <= p, 0.0, -BIG)
    c[:, C_P2F] = np.where(f < p, 0.0, BIG)
    c[:, C_P2B] = np.where(f > p, 0.0, BIG)
    return c


def _blk(w, c0, n=128):
    return np.ascontiguousarray(w[:, c0:c0 + n].reshape(KC, 128, n).transpose(1, 0, 2))


def prep_inputs(x, norm_w, w_in, sgu_ln_g, sgu_ln_b, sgu_w, sgu_b, conv_w, a_log_f, a_log_b, dt_bias_f,
                dt_bias_b, dn_norm_w, w_out, final_norm_w):
    f = np.float32
    x = np.asarray(x, f)
    w_in = np.asarray(w_in, f)
    w_out = np.asarray(w_out, f)
    rep = lambda v, n=128: np.ascontiguousarray(np.broadcast_to(np.asarray(v, f)[None], (n,) + np.asarray(v).shape))
    shared = {}
    shared["normw"] = np.ascontiguousarray(np.stack([np.asarray(norm_w[l], f).reshape(KC, 128).T for l in range(L)]))
    cols = [1024 + 128 * g for g in range(8)]
    for g in range(8):
        cols += [128 * g, 2048 + 128 * g]
    cols += [6144 + 128 * j for j in range(8)]
    shared["wsg"] = np.ascontiguousarray(np.stack([np.stack([_blk(w_in[l], c) for c in cols]) for l in range(L)]))
    shared["wout"] = np.ascontiguousarray(
        np.stack([np.stack([_blk(w_out[l], j * 512, 512) for j in range(4)]) for l in range(L)]))
    shared["lng"] = np.stack([rep(sgu_ln_g[l]) for l in range(L)])
    shared["lnb"] = np.stack([rep(sgu_ln_b[l]) for l in range(L)])
    shared["sguwT"] = np.ascontiguousarray(np.stack([np.asarray(sgu_w[l], f).transpose(2, 0, 1) for l in range(L)]))
    shared["sgub"] = np.ascontiguousarray(np.stack([rep(np.asarray(sgu_b[l], f)) for l in range(L)]))
    shared["dnw"] = np.ascontiguousarray(np.asarray(dn_norm_w, f).T)
    shared["fnw"] = rep(final_norm_w)
    shared["consts"] = _consts()
    in_maps = []
    for c in range(8):
        b, s = c // 4, c % 4
        hs = (2 * s, 2 * s + 1)
        m = dict(shared)
        m["x"] = np.ascontiguousarray(x[b, s * T:(s + 1) * T, :])
        m["wdn"] = np.ascontiguousarray(np.stack([np.stack(
            [_blk(w_in[l], 3072 + ty * 1024 + h * 128) for h in hs for ty in range(3)]) for l in range(L)]))
        gcols = [7168 + hs[0], 7176 + hs[0], 7168 + hs[1], 7176 + hs[1],
                 7184 + hs[0], 7192 + hs[0], 7184 + hs[1], 7192 + hs[1]]
        m["wdg"] = np.ascontiguousarray(np.stack(
            [w_in[l][:, gcols].reshape(KC, 128, 8).transpose(1, 0, 2) for l in range(L)]))
        cw = np.asarray(conv_w, f)
        m["convw"] = np.ascontiguousarray(np.stack([np.stack(
            [cw[l][:, ty * 1024 + h * 128: ty * 1024 + (h + 1) * 128].T for h in hs for ty in range(3)], axis=1)
            for l in range(L)]))
        al = np.stack([np.array([a_log_f[l][hs[0]], a_log_b[l][hs[0]], a_log_f[l][hs[1]], a_log_b[l][hs[1]]], f)
                       for l in range(L)])
        db = np.stack([np.array([dt_bias_f[l][hs[0]], dt_bias_b[l][hs[0]], dt_bias_f[l][hs[1]], dt_bias_b[l][hs[1]]], f)
                       for l in range(L)])
        m["alog"] = np.ascontiguousarray(np.broadcast_to(al[:, None, :], (L, 128, 4)))
        m["dtb"] = np.ascontiguousarray(np.broadcast_to(db[:, None, :], (L, 128, 4)))
        sl = np.zeros((128, 4), f)
        sl[:, s] = 1.0
        m["sel"] = sl
        in_maps.append(m)
    return in_maps


_LAST = {}


def kernel(**inputs):
    inputs = {k: np.asarray(v) for k, v in inputs.items()}
    in_maps = prep_inputs(**inputs)
    depth = int(os.environ.get("MK_DEPTH", L))
    nc = build(depth)
    res = run_bass_kernel_spmd(nc, in_maps, core_ids=list(range(8)))
    if DBG:
        _LAST["res"] = res.results[0]
    outp = np.zeros((2, SEQ, D), np.float32)
    for c in range(8):
        b, s = c // 4, c % 4
        outp[b, s * T:(s + 1) * T, :] = res.results[c]["out"]
    return outp
```
